# Optimizing a Trainium2 kernel written in Bass

```python
import jax, jax.numpy as jnp
from jax import lax
import numpy as np

D_MODEL = 1024
BATCH = 4
SEQ = 4096
DEPTH = 1

N_HEADS = 16
HEAD_DIM = 64
N_KV = 4
HPG = N_HEADS // N_KV
ATTN_DIM = N_HEADS * HEAD_DIM
KV_DIM = N_KV * HEAD_DIM
CMP_BLOCK = 32
CMP_STRIDE = 16
CMP_HIDDEN = 256
SEL_BLOCK = 64
N_SEL = 16
WINDOW = 512
Q_BLOCK = 128
FORCE_BONUS = 1e4
LRU_WIDTH = 1024
LRU_BLOCKS = 16
LRU_BW = LRU_WIDTH // LRU_BLOCKS
CONV_W = 4
LRU_C = 8.0
D_FF = -(-8 * D_MODEL // 768) * 256
IN_DIM = ATTN_DIM + 6 * KV_DIM + 3 * N_HEADS + 2 * LRU_WIDTH + 2 * D_MODEL
EPS = 1e-6

kernel_name = "hybrid_nsa_rglru_gated_block"


def _rmsnorm(x, g):
    xf = x.astype(jnp.float32)
    y = xf * lax.rsqrt(jnp.mean(xf * xf, axis=-1, keepdims=True) + EPS)
    return (y * g.astype(jnp.float32)).astype(x.dtype)


def _masked_softmax(s, mask):
    s = jnp.where(mask, s.astype(jnp.float32), -1e30)
    m = jnp.max(s, axis=-1, keepdims=True)
    e = jnp.exp(s - m) * mask
    return e / jnp.maximum(jnp.sum(e, axis=-1, keepdims=True), 1e-30)


def _alibi_slopes():
    return np.power(2.0, -8.0 * np.arange(1, N_HEADS + 1) / N_HEADS).astype(np.float32)


def _cmp_to_sel_map(n_cmp, n_blk):
    cs = np.arange(n_cmp) * CMP_STRIDE
    ce = cs + CMP_BLOCK - 1
    bs = np.arange(n_blk) * SEL_BLOCK
    be = bs + SEL_BLOCK - 1
    return ((cs[:, None] <= be[None, :]) & (ce[:, None] >= bs[None, :])).astype(np.float32)


def _compress(kv, pos, w1, b1, w2, b2):
    B, T, G, D = kv.shape
    chunks = kv.reshape(B, T // CMP_STRIDE, CMP_STRIDE, G, D)
    blocks = jnp.concatenate([chunks[:, :-1], chunks[:, 1:]], axis=2) + pos[None, None, :, None, :]
    hid = jax.nn.gelu(jnp.einsum('bnlgd,ldf->bngf', blocks, w1) + b1)
    return jnp.einsum('bngf,fd->bngd', hid, w2) + b2


def _nsa(q, k_cmp, v_cmp, k_slc, v_slc, k_win, v_win, gates):
    B, T = q.shape[0], q.shape[1]
    n_cmp = k_cmp.shape[1]
    n_blk = T // SEL_BLOCK
    n_sel = min(N_SEL, n_blk)
    n_tok = n_sel * SEL_BLOCK
    scale = HEAD_DIM ** -0.5
    slopes = jnp.asarray(_alibi_slopes()).reshape(1, N_KV, HPG, 1, 1)
    cmp_end = jnp.arange(n_cmp) * CMP_STRIDE + CMP_BLOCK - 1
    blk_map = jnp.asarray(_cmp_to_sel_map(n_cmp, n_blk))
    kb = k_slc.reshape(B, n_blk, SEL_BLOCK, N_KV, HEAD_DIM).transpose(0, 3, 1, 2, 4)
    vb = v_slc.reshape(B, n_blk, SEL_BLOCK, N_KV, HEAD_DIM).transpose(0, 3, 1, 2, 4)
    k_pad = jnp.pad(k_win, ((0, 0), (WINDOW, 0), (0, 0), (0, 0)))
    v_pad = jnp.pad(v_win, ((0, 0), (WINDOW, 0), (0, 0), (0, 0)))
    b_ix = jnp.arange(B)[:, None, None, None]
    g_ix = jnp.arange(N_KV)[None, :, None, None]
    blk_ids = jnp.arange(n_blk)

    def block(i):
        qs = i * Q_BLOCK
        qb = lax.dynamic_slice_in_dim(q, qs, Q_BLOCK, axis=1)
        gb = lax.dynamic_slice_in_dim(gates, qs, Q_BLOCK, axis=1)
        t = qs + jnp.arange(Q_BLOCK)
        dist = (t[:, None] - cmp_end[None, :]).astype(jnp.float32)
        s = jnp.einsum('bqghd,bngd->bghqn', qb, k_cmp) * scale
        p_cmp = _masked_softmax(s - slopes * dist, dist >= 0)
        o_cmp = jnp.einsum('bghqn,bngd->bqghd', p_cmp.astype(v_cmp.dtype), v_cmp)
        imp = jnp.einsum('bghqn,nj->bgqj', p_cmp, blk_map)
        cur = (t // SEL_BLOCK)[:, None]
        forced = ((blk_ids == 0) | (blk_ids == cur) | (blk_ids == cur - 1)).astype(jnp.float32)
        valid = blk_ids[None, :] * SEL_BLOCK <= t[:, None]
        imp = jnp.where(valid, imp + FORCE_BONUS * forced, -1e30)
        _, idx = lax.top_k(imp, n_sel)
        k_sel = kb[b_ix, g_ix, idx].reshape(B, N_KV, Q_BLOCK, n_tok, HEAD_DIM)
        v_sel = vb[b_ix, g_ix, idx].reshape(B, N_KV, Q_BLOCK, n_tok, HEAD_DIM)
        s_pos = (idx[..., None] * SEL_BLOCK + jnp.arange(SEL_BLOCK)).reshape(B, N_KV, 1, Q_BLOCK, n_tok)
        dist = (t[None, None, None, :, None] - s_pos).astype(jnp.float32)
        s = jnp.einsum('bqghd,bgqmd->bghqm', qb, k_sel) * scale
        p = _masked_softmax(s - slopes * dist, dist >= 0)
        o_slc = jnp.einsum('bghqm,bgqmd->bqghd', p.astype(v_sel.dtype), v_sel)
        kw = lax.dynamic_slice_in_dim(k_pad, qs, WINDOW + Q_BLOCK, axis=1)
        vw = lax.dynamic_slice_in_dim(v_pad, qs, WINDOW + Q_BLOCK, axis=1)
        w_pos = qs - WINDOW + jnp.arange(WINDOW + Q_BLOCK)
        wd = t[:, None] - w_pos[None, :]
        wmask = (wd >= 0) & (wd < WINDOW) & (w_pos[None, :] >= 0)
        s = jnp.einsum('bqghd,bkgd->bghqk', qb, kw) * scale
        p = _masked_softmax(s - slopes * wd.astype(jnp.float32), wmask)
        o_win = jnp.einsum('bghqk,bkgd->bqghd', p.astype(vw.dtype), vw)
        o = gb[..., 0:1] * o_cmp + gb[..., 1:2] * o_slc + gb[..., 2:3] * o_win
        return o.reshape(B, Q_BLOCK, ATTN_DIM)

    out = lax.map(block, jnp.arange(T // Q_BLOCK))
    return out.transpose(1, 0, 2, 3).reshape(B, T, ATTN_DIM)


def _lin_combine(c1, c2):
    a1, b1 = c1
    a2, b2 = c2
    return a1 * a2, a2 * b1 + b2


def _rglru_branch(xb, gate, conv_w, conv_b, wa, ba, wi, bi, lam):
    B, T, W = xb.shape
    xc = lax.conv_general_dilated(xb, conv_w, window_strides=(1,), padding=[(CONV_W - 1, 0)],
                                  dimension_numbers=('NWC', 'WIO', 'NWC'),
                                  feature_group_count=LRU_WIDTH) + conv_b
    xr = xc.reshape(B, T, LRU_BLOCKS, LRU_BW)
    r = jax.nn.sigmoid(jnp.einsum('btnk,nkj->btnj', xr, wa).reshape(B, T, W) + ba)
    i = jax.nn.sigmoid(jnp.einsum('btnk,nkj->btnj', xr, wi).reshape(B, T, W) + bi)
    log_a = -LRU_C * r.astype(jnp.float32) * jax.nn.softplus(-lam.astype(jnp.float32))
    a = jnp.exp(log_a)
    b = jnp.sqrt(-jnp.expm1(2.0 * log_a)) * (i * xc).astype(jnp.float32)
    _, h = lax.associative_scan(_lin_combine, (a, b), axis=1)
    return h.astype(xb.dtype) * jax.nn.gelu(gate)


def _layer(x, norm1_g, w_in, q_norm_g, k_norm_g, cmp_pos_k, cmp_w1_k, cmp_b1_k, cmp_w2_k, cmp_b2_k,
           cmp_pos_v, cmp_w1_v, cmp_b1_v, cmp_w2_v, cmp_b2_v, conv_w, conv_b, lru_wa, lru_ba,
           lru_wi, lru_bi, lru_lambda, w_o_attn, w_o_lru, w_out, norm2_g, w_gate, w_up, w_down):
    B, T, _ = x.shape
    h = _rmsnorm(x, norm1_g)
    proj = h @ w_in
    o1 = ATTN_DIM
    o2 = o1 + 6 * KV_DIM
    o3 = o2 + 3 * N_HEADS
    o4 = o3 + LRU_WIDTH
    o5 = o4 + LRU_WIDTH
    q = _rmsnorm(proj[..., :o1].reshape(B, T, N_KV, HPG, HEAD_DIM), q_norm_g)
    kv = proj[..., o1:o2].reshape(B, T, 6, N_KV, HEAD_DIM)
    nsa_g = jax.nn.sigmoid(proj[..., o2:o3].reshape(B, T, N_KV, HPG, 3))
    lru_x = proj[..., o3:o4]
    lru_gate = proj[..., o4:o5]
    merge_g = jax.nn.sigmoid(proj[..., o5:].reshape(B, T, 2, D_MODEL))
    k_cmp = _rmsnorm(_compress(kv[:, :, 0], cmp_pos_k, cmp_w1_k, cmp_b1_k, cmp_w2_k, cmp_b2_k), k_norm_g[0])
    v_cmp = _compress(kv[:, :, 1], cmp_pos_v, cmp_w1_v, cmp_b1_v, cmp_w2_v, cmp_b2_v)
    k_slc = _rmsnorm(kv[:, :, 2], k_norm_g[1])
    k_win = _rmsnorm(kv[:, :, 4], k_norm_g[2])
    attn = _nsa(q, k_cmp, v_cmp, k_slc, kv[:, :, 3], k_win, kv[:, :, 5], nsa_g)
    lru = _rglru_branch(lru_x, lru_gate, conv_w, conv_b, lru_wa, lru_ba, lru_wi, lru_bi, lru_lambda)
    merged = merge_g[:, :, 0] * (attn @ w_o_attn) + merge_g[:, :, 1] * (lru @ w_o_lru)
    x = x + merged @ w_out
    h2 = _rmsnorm(x, norm2_g)
    return x + (jax.nn.silu(h2 @ w_gate) * (h2 @ w_up)) @ w_down


def setup_inputs(seed: int = 0) -> dict:
    key = jax.random.key(seed)
    ks = jax.random.split(key, 32)
    L = DEPTH
    nrm = lambda k, shape, fan: jax.random.normal(k, shape, jnp.float32) * (fan ** -0.5)
    small = lambda k, shape, s: jax.random.normal(k, shape, jnp.float32) * s
    u = jax.random.uniform(ks[31], (L, LRU_WIDTH), jnp.float32, minval=0.9, maxval=0.999)
    a0 = u ** (1.0 / LRU_C)
    return {
        "x": jax.random.normal(ks[0], (BATCH, SEQ, D_MODEL), jnp.float32),
        "norm1_g": 1.0 + small(ks[1], (L, D_MODEL), 0.02),
        "w_in": nrm(ks[2], (L, D_MODEL, IN_DIM), D_MODEL),
        "q_norm_g": 1.0 + small(ks[3], (L, HEAD_DIM), 0.02),
        "k_norm_g": 1.0 + small(ks[4], (L, 3, HEAD_DIM), 0.02),
        "cmp_pos_k": small(ks[5], (L, CMP_BLOCK, HEAD_DIM), 0.1),
        "cmp_w1_k": nrm(ks[6], (L, CMP_BLOCK, HEAD_DIM, CMP_HIDDEN), CMP_BLOCK * HEAD_DIM),
        "cmp_b1_k": small(ks[7], (L, CMP_HIDDEN), 0.01),
        "cmp_w2_k": nrm(ks[8], (L, CMP_HIDDEN, HEAD_DIM), CMP_HIDDEN),
        "cmp_b2_k": small(ks[9], (L, HEAD_DIM), 0.01),
        "cmp_pos_v": small(ks[10], (L, CMP_BLOCK, HEAD_DIM), 0.1),
        "cmp_w1_v": nrm(ks[11], (L, CMP_BLOCK, HEAD_DIM, CMP_HIDDEN), CMP_BLOCK * HEAD_DIM),
        "cmp_b1_v": small(ks[12], (L, CMP_HIDDEN), 0.01),
        "cmp_w2_v": nrm(ks[13], (L, CMP_HIDDEN, HEAD_DIM), CMP_HIDDEN),
        "cmp_b2_v": small(ks[14], (L, HEAD_DIM), 0.01),
        "conv_w": nrm(ks[15], (L, CONV_W, 1, LRU_WIDTH), CONV_W),
        "conv_b": small(ks[16], (L, LRU_WIDTH), 0.01),
        "lru_wa": nrm(ks[17], (L, LRU_BLOCKS, LRU_BW, LRU_BW), LRU_BW),
        "lru_ba": small(ks[18], (L, LRU_WIDTH), 0.01),
        "lru_wi": nrm(ks[19], (L, LRU_BLOCKS, LRU_BW, LRU_BW), LRU_BW),
        "lru_bi": small(ks[20], (L, LRU_WIDTH), 0.01),
        "lru_lambda": jnp.log(a0) - jnp.log1p(-a0),
        "w_o_attn": nrm(ks[21], (L, ATTN_DIM, D_MODEL), ATTN_DIM),
        "w_o_lru": nrm(ks[22], (L, LRU_WIDTH, D_MODEL), LRU_WIDTH),
        "w_out": nrm(ks[23], (L, D_MODEL, D_MODEL), D_MODEL),
        "norm2_g": 1.0 + small(ks[24], (L, D_MODEL), 0.02),
        "w_gate": nrm(ks[25], (L, D_MODEL, D_FF), D_MODEL),
        "w_up": nrm(ks[26], (L, D_MODEL, D_FF), D_MODEL),
        "w_down": nrm(ks[27], (L, D_FF, D_MODEL), D_FF),
    }


def reference(x, norm1_g, w_in, q_norm_g, k_norm_g, cmp_pos_k, cmp_w1_k, cmp_b1_k, cmp_w2_k, cmp_b2_k,
              cmp_pos_v, cmp_w1_v, cmp_b1_v, cmp_w2_v, cmp_b2_v, conv_w, conv_b, lru_wa, lru_ba,
              lru_wi, lru_bi, lru_lambda, w_o_attn, w_o_lru, w_out, norm2_g, w_gate, w_up, w_down):
    for l in range(DEPTH):
        x = _layer(x, norm1_g[l], w_in[l], q_norm_g[l], k_norm_g[l], cmp_pos_k[l], cmp_w1_k[l],
                   cmp_b1_k[l], cmp_w2_k[l], cmp_b2_k[l], cmp_pos_v[l], cmp_w1_v[l], cmp_b1_v[l],
                   cmp_w2_v[l], cmp_b2_v[l], conv_w[l], conv_b[l], lru_wa[l], lru_ba[l], lru_wi[l],
                   lru_bi[l], lru_lambda[l], w_o_attn[l], w_o_lru[l], w_out[l], norm2_g[l],
                   w_gate[l], w_up[l], w_down[l])
    return x
```

```python
from contextlib import ExitStack
import numpy as np
import ml_dtypes
import concourse.bass as bass
import concourse.mybir as mybir
from concourse.bass_utils import run_bass_kernel_spmd

F32 = mybir.dt.float32
BF16 = mybir.dt.bfloat16
AF = mybir.ActivationFunctionType
ALU = mybir.AluOpType
AX = mybir.AxisListType
NPBF = ml_dtypes.bfloat16

D = 1024
T = 4096
TO = 2048
NT = T // 128
IN_DIM = 6704
O_KV = 1024
O_G = 2560
O_LX = 2608
O_LG = 3632
O_MG = 4656
DFF = 2816
EPS = 1e-6
QT_LIMIT = 0
DEFER = [9, 14, 5]
SKIP = ()

COMPUTE = ("tensor", "vector", "scalar", "gpsimd")


class Sched:
    def __init__(self, nc, stack, n_dma_sems=20):
        self.nc = nc
        self.csem = {e: stack.enter_context(nc.semaphore("c_" + e)) for e in COMPUTE}
        self.ccount = {e: 0 for e in COMPUTE}
        self.dsem, self.dround, self.dnext = {}, {}, {}
        for q in ("sync", "gpsimd"):
            self.dsem[q] = [stack.enter_context(nc.semaphore(f"d_{q}{i}")) for i in range(n_dma_sems)]
            self.dround[q] = [0] * n_dma_sems
            self.dnext[q] = 0
        self.waited = {}
        self.reset_block()

    def reset_block(self):
        self.ops = []
        self.last_w = {}
        self.readers = {}

    def _add(self, eng, kind, fn, reads, writes, nosync_same=False):
        deps = []
        for k in reads:
            if k in self.last_w:
                deps.append(self.last_w[k])
        for k in writes:
            if k in self.last_w:
                deps.append(self.last_w[k])
            deps.extend(self.readers.get(k, ()))
        pre = None
        if kind == "dma":
            i = self.dnext[eng]
            self.dnext[eng] = (i + 1) % len(self.dsem[eng])
            sem = self.dsem[eng][i]
            if self.dround[eng][i] > 0:
                pre = (sem, 16 * self.dround[eng][i], None)
            self.dround[eng][i] += 1
            tok = (sem, 16 * self.dround[eng][i], None)
            inc = 16
        else:
            self.ccount[eng] += 1
            tok = (self.csem[eng], self.ccount[eng], eng)
            inc = 1
        dd = [d for d in deps if not (nosync_same and d[2] == eng and kind != "dma")]
        if pre is not None:
            dd.append(pre)
        self.ops.append((eng, fn, dd, tok[0], inc))
        for k in writes:
            self.last_w[k] = tok
            self.readers[k] = []
        for k in reads:
            self.readers.setdefault(k, []).append(tok)
        return tok

    def pe(self, fn, reads=(), writes=()):
        return self._add("tensor", "c", fn, reads, writes, nosync_same=True)

    def act(self, fn, reads=(), writes=()):
        return self._add("scalar", "c", fn, reads, writes)

    def dve(self, fn, reads=(), writes=()):
        return self._add("vector", "c", fn, reads, writes)

    def pool(self, fn, reads=(), writes=()):
        return self._add("gpsimd", "c", fn, reads, writes)

    def dma(self, fn, reads=(), writes=(), q="sync"):
        return self._add(q, "dma", fn, reads, writes)

    def emit(self, final_tokens=()):
        nc = self.nc
        per = {}
        for op in self.ops:
            per.setdefault(op[0], []).append(op)
        waited = self.waited

        def run(engname, eng, extra):
            for (_, fn, deps, sem, inc) in per.get(engname, []):
                need = {}
                for (s, v, _) in deps:
                    if need.get(id(s), (None, 0))[1] < v:
                        need[id(s)] = (s, v)
                for key, (s, v) in need.items():
                    if waited.get((engname, key), 0) >= v:
                        continue
                    eng.wait_ge(s, v)
                    waited[(engname, key)] = v
                fn(eng).then_inc(sem, inc)
            for (s, v, _) in extra:
                eng.wait_ge(s, v)

        drain = []
        for q in ("sync", "gpsimd"):
            for i, sem in enumerate(self.dsem[q]):
                if self.dround[q][i] > 0:
                    drain.append((sem, 16 * self.dround[q][i], None))
        extra = list(final_tokens) + drain
        with nc.Block() as block:
            names = list(per.keys())
            if "sync" not in names:
                names.append("sync")
            for engname in names:
                def f(eng, engname=engname):
                    run(engname, eng, extra if engname == "sync" else ())
                getattr(block, engname)(f)
        self.reset_block()


def build_nc(upto="all", debug=()):
    nc = bass.Bass("TRN2", target_bir_lowering=False)
    st = ExitStack()

    def din(name, shape, dt=F32):
        return nc.dram_tensor(name, list(shape), dt, kind="ExternalInput").ap()

    def dscratch(name, shape, dt=F32):
        kind = "ExternalOutput" if name in debug else "Internal"
        return nc.dram_tensor(name, list(shape), dt, kind=kind).ap()

    xv = din("xv", [T, D])
    w_in = din("w_in", [D, IN_DIM])
    g1_in = din("g1", [128, 8])
    ident_in = din("ident", [128, 128], BF16)
    qg_in = din("qg", [64, 4])
    y = nc.dram_tensor("y", [TO, D], F32, kind="ExternalOutput").ap()
    posT_in = din("posT", [2, 64, 32])
    w1_in = din("cw1", [2, 32, 64, 256])
    b1_in = din("cb1", [2, 128, 2])
    w2_in = din("cw2", [2, 128, 2, 64])
    b2c_in = din("cb2c", [2, 64, 1])
    b2r_in = din("cb2r", [2, 1, 64])
    kx_in = din("kx", [7, T], BF16)
    kcx_in = din("kcx", [7, 256], BF16)
    qx_in = din("qx", [16, 7, TO], BF16)
    esel_in = din("esel", [128, 32, 128], BF16)
    cmask_in = din("cmask", [128, 16, 2, 128], BF16)
    mfar_in = din("mfar", [128, 128], BF16)
    mdiag_in = din("mdiag", [128, 128], BF16)
    bmap_in = din("bmap", [128, 2, 65], BF16)
    bonus_in = din("bonus", [128, 16, 64])
    hb_in = din("hb", [128, 1])
    hf_in = din("hf", [128, 1])
    woa_in = din("w_o_attn", [D, D])
    wol_in = din("w_o_lru", [D, D])
    wout_in = din("w_out", [D, D])
    g2_in = din("g2", [128, 8])
    wg_in = din("w_gate", [D, DFF])
    wu_in = din("w_up", [D, DFF])
    wd_in = din("w_down", [DFF, D])
    lcw_in = din("lcw", [128, 8, 4])
    lpar_in = din("lpar", [128, 4, 8])
    lbd_in = din("lbd", [2, 8, 128, 128])

    qT_s = dscratch("qT_s", [16, 64, TO], BF16)
    kT_s = dscratch("kT_s", [2, 4, 64, T], BF16)
    ckT_s = dscratch("ckT_s", [2, 4, 64, T], BF16)
    v_s = dscratch("v_s", [2, T, 256], BF16)
    gate_s = dscratch("gate_s", [48, TO], F32)
    lx_s = dscratch("lx_s", [8, 128, T], F32)
    lg_s = dscratch("lg_s", [8, 128, TO], F32)
    mg_s = dscratch("mg_s", [TO, 2048], F32)
    kcT_s = dscratch("kcT_s", [4, 64, 256], BF16)
    vc_s = dscratch("vc_s", [256, 4, 64], BF16)
    attnT_s = dscratch("attnT_s", [16, 64, TO], BF16)
    lruT_s = dscratch("lruT_s", [8, 128, TO], BF16)
    x1_s = dscratch("x1_s", [TO, D], F32)

    sc = Sched(nc, st)
    sb = lambda name, shape, dt=F32: st.enter_context(nc.sbuf_tensor("sb_" + name, list(shape), dt))
    psA = st.enter_context(nc.psum_tensor("psA", [128, 6, 512], F32))
    psT = st.enter_context(nc.psum_tensor("psT", [128, 1, 1024], BF16))
    psT_f = st.enter_context(nc.psum_tensor("psT_f", [128, 4, 65], F32))

    ident = sb("ident", [128, 128], BF16)
    g1 = sb("g1sb", [128, 8])
    qg = sb("qgsb", [64, 4])
    ones64 = sb("ones64", [64, 64])
    epsb = sb("epsb", [128, 1])
    hst = ExitStack()
    hT = hst.enter_context(nc.sbuf_tensor("sb_hT", [128, 8, T], BF16))

    last_out_tokens = []

    sc.dma(lambda e: e.dma_start(out=ident[:, :], in_=ident_in), writes=["ident"])
    sc.dma(lambda e: e.dma_start(out=g1[:, :], in_=g1_in), writes=["g1"])
    sc.dma(lambda e: e.dma_start(out=qg[:, :], in_=qg_in), writes=["qg"])
    sc.dve(lambda e: e.memset(ones64[:, :], 1.0), writes=["ones64"])
    sc.dve(lambda e: e.memset(epsb[:, :], EPS), writes=["epsb"])

    with ExitStack() as pa:
        sba = lambda name, shape, dt=F32: pa.enter_context(nc.sbuf_tensor("sa_" + name, list(shape), dt))
        xt = [sba(f"xt{i}", [128, D]) for i in range(3)]
        sq = [sba(f"sq{i}", [128, D]) for i in range(2)]
        xn = [sba(f"xn{i}", [128, D], BF16) for i in range(2)]
        ss = [sba(f"ss{i}", [128, 1]) for i in range(2)]
        for t in range(NT):
            a, b2 = t % 3, t % 2
            sc.dma(lambda e, a=a, t=t: e.dma_start(out=xt[a][:, :], in_=xv[t * 128:(t + 1) * 128, :]),
                   writes=[("xt", a)])
            sc.dve(lambda e, a=a, b2=b2: e.tensor_tensor(out=sq[b2][:, :], in0=xt[a][:, :], in1=xt[a][:, :], op=ALU.mult),
                   reads=[("xt", a)], writes=[("sq", b2)])
            sc.dve(lambda e, b2=b2: e.tensor_reduce(out=ss[b2][:, :], in_=sq[b2][:, :], axis=AX.X, op=ALU.add),
                   reads=[("sq", b2)], writes=[("ss", b2)])
            sc.dve(lambda e, b2=b2: e.tensor_scalar(out=ss[b2][:, :], in0=ss[b2][:, :], scalar1=1.0 / D, scalar2=EPS,
                                                    op0=ALU.mult, op1=ALU.add),
                   reads=[("ss", b2)], writes=[("ss", b2)])
            sc.dve(lambda e, b2=b2: e.reciprocal(out=ss[b2][:, :], in_=ss[b2][:, :]),
                   reads=[("ss", b2)], writes=[("ss", b2)])
            sc.act(lambda e, b2=b2: e.activation(out=ss[b2][:, :], in_=ss[b2][:, :], func=AF.Sqrt),
                   reads=[("ss", b2)], writes=[("ss", b2)])
            sc.act(lambda e, a=a, b2=b2: e.mul(out=xn[b2][:, :], in_=xt[a][:, :], mul=ss[b2][:, 0:1]),
                   reads=[("xt", a), ("ss", b2)], writes=[("xn", b2)])
            for kc in range(8):
                sc.pe(lambda e, b2=b2, kc=kc: e.transpose(out=psT[:, 0, kc * 128:(kc + 1) * 128],
                                                          in_=xn[b2][:, kc * 128:(kc + 1) * 128], identity=ident[:, :]),
                      reads=[("xn", b2), "ident"], writes=[("psT", 0)])
            cp = sc.dve if t % 2 == 0 else sc.act
            if t % 2 == 0:
                sc.dve(lambda e, b2=b2, t=t: e.tensor_copy(out=hT[:, :, t * 128:(t + 1) * 128],
                                                           in_=psT[:, 0, :].rearrange("p (k n) -> p k n", k=8)),
                       reads=[("psT", 0)], writes=[("hT", t // 4)])
            else:
                sc.act(lambda e, b2=b2, t=t: e.copy(out=hT[:, :, t * 128:(t + 1) * 128],
                                                    in_=psT[:, 0, :].rearrange("p (k n) -> p k n", k=8)),
                       reads=[("psT", 0)], writes=[("hT", t // 4)])
        if upto == "A":
            dbg = dscratch("hT_dbg", [128, 8, T], BF16)
            tk = sc.dma(lambda e: e.dma_start(out=dbg, in_=hT[:, :, :]), reads=[("hT", i) for i in range(8)])
            sc.emit([tk])
            pa.close()
            hst.close()
            st.close()
            return nc
        sc.emit()

    with ExitStack() as pb:
        sbb = lambda name, shape, dt=F32: pb.enter_context(nc.sbuf_tensor("sbb_" + name, list(shape), dt))
        wf = [sbb(f"wf{i}", [128, 8, 128]) for i in range(2)]
        wb = [sbb(f"wb{i}", [128, 8, 128], BF16) for i in range(2)]
        wf2 = [sbb(f"wf2{i}", [128, 8, 512]) for i in range(2)]
        wb2 = [sbb(f"wb2{i}", [128, 8, 512], BF16) for i in range(2)]
        ev = [sbb(f"ev{i}", [128, 512]) for i in range(3)]
        evb = [sbb(f"evb{i}", [128, 512], BF16) for i in range(3)]
        sqbb = [sbb(f"sqbb{i}", [128, 512], BF16) for i in range(2)]
        ones128b = sbb("ones128b", [128, 64], BF16)
        sc.dve(lambda e: e.memset(ones128b[:, :], 1.0), writes=["ones128b"])
        for i in range(2):
            sc.pool(lambda e, i=i: e.memset(sqbb[i][64:128, :], 0.0), writes=[("sqz", i)])
        rsb = [sbb(f"rsb{i}", [64, 512]) for i in range(2)]
        w_in_v = w_in.rearrange("(kc p) n -> p kc n", p=128)
        cnt = {"w": 0, "ps": 0, "ev": 0, "n": 0, "w2": 0}

        def load_w(col0, M):
            s = cnt["w"] % 2
            cnt["w"] += 1
            sc.dma(lambda e: e.dma_start(out=wf[s][:, :, 0:M], in_=w_in_v[:, :, col0:col0 + M]), writes=[("wf", s)])
            sc.pool(lambda e: e.tensor_tensor(out=wb[s][:, :, 0:M], in0=wf[s][:, :, 0:M],
                                              in1=g1[:, :].unsqueeze(2).to_broadcast([128, 8, M]), op=ALU.mult),
                    reads=[("wf", s), "g1"], writes=[("wb", s)])
            return s

        def mm_fm(s, M, tb):
            p = cnt["ps"] % 4
            cnt["ps"] += 1
            for kc in range(8):
                sc.pe(lambda e, kc=kc: e.matmul(psA[0:M, p, :], lhsT=wb[s][:, kc, 0:M], rhs=hT[:, kc, tb * 512:(tb + 1) * 512],
                                                start=(kc == 0), stop=(kc == 7)),
                      reads=[("wb", s), ("hT", tb)], writes=[("psA", p)])
            flush_pending()
            return p

        pending = []

        def flush_pending():
            while pending:
                pending.pop(0)()

        def evac_norm(p, gcol, dst):
            n = cnt["n"] % 2
            cnt["n"] += 1
            e3 = cnt["ev"] % 3
            cnt["ev"] += 1
            sc.act(lambda e: e.activation(out=sqbb[n][0:64, :], in_=psA[0:64, p, :], func=AF.Square),
                   reads=[("psA", p), ("sqz", n)], writes=[("sqb", n)])

            def part_b():
                sc.pe(lambda e: e.matmul(psA[0:64, 4 + n, :], lhsT=ones128b[:, :], rhs=sqbb[n][:, :], start=True, stop=True),
                      reads=[("sqb", n), ("sqz", n), "ones128b"], writes=[("psA", 4 + n)])
                sc.act(lambda e: e.activation(out=rsb[n][:, :], in_=psA[0:64, 4 + n, :], func=AF.Ln, scale=1.0 / 64, bias=epsb[0:64, 0:1]),
                       reads=[("psA", 4 + n), "epsb"], writes=[("rsb", n)])
                sc.act(lambda e: e.activation(out=rsb[n][:, :], in_=rsb[n][:, :], func=AF.Exp, scale=-0.5),
                       reads=[("rsb", n)], writes=[("rsb", n)])
                sc.dve(lambda e: e.scalar_tensor_tensor(out=evb[e3][0:64, :], in0=psA[0:64, p, :], scalar=qg[:, gcol:gcol + 1],
                                                        in1=rsb[n][:, :], op0=ALU.mult, op1=ALU.mult),
                       reads=[("psA", p), ("rsb", n), "qg"], writes=[("evb", e3)])
                sc.dma(lambda e: e.dma_start(out=dst, in_=evb[e3][0:64, :]), reads=[("evb", e3)], q="gpsimd")
            pending.append(part_b)

        def evac_copy_bf(p, M, dst):
            e3 = cnt["ev"] % 3
            cnt["ev"] += 1
            sc.act(lambda e: e.copy(out=evb[e3][0:M, :], in_=psA[0:M, p, :]), reads=[("psA", p)], writes=[("evb", e3)])
            sc.dma(lambda e: e.dma_start(out=dst, in_=evb[e3][0:M, :]), reads=[("evb", e3)], q="gpsimd")

        def evac_f32(p, M, dst, func=None):
            e3 = cnt["ev"] % 3
            cnt["ev"] += 1
            if func is None:
                sc.dve(lambda e: e.tensor_copy(out=ev[e3][0:M, :], in_=psA[0:M, p, :]), reads=[("psA", p)], writes=[("ev", e3)])
            else:
                sc.act(lambda e: e.activation(out=ev[e3][0:M, :], in_=psA[0:M, p, :], func=func),
                       reads=[("psA", p)], writes=[("ev", e3)])
            sc.dma(lambda e: e.dma_start(out=dst, in_=ev[e3][0:M, :]), reads=[("ev", e3)], q="gpsimd")

        for h in range(16):
            s = load_w(h * 64, 64)
            for tb in range(4, 8):
                p = mm_fm(s, 64, tb)
                evac_norm(p, 0, qT_s[h, :, (tb - 4) * 512:(tb - 3) * 512])
        for si, (kset, gcol) in enumerate(((2, 2), (4, 3))):
            for g in range(4):
                s = load_w(O_KV + kset * 256 + g * 64, 64)
                for tb in range(8):
                    p = mm_fm(s, 64, tb)
                    evac_norm(p, gcol, kT_s[si, g, :, tb * 512:(tb + 1) * 512])
        for si in range(2):
            for g in range(4):
                s = load_w(O_KV + si * 256 + g * 64, 64)
                for tb in range(8):
                    p = mm_fm(s, 64, tb)
                    evac_copy_bf(p, 64, ckT_s[si, g, :, tb * 512:(tb + 1) * 512])
        s = load_w(O_G, 48)
        for tb in range(4, 8):
            p = mm_fm(s, 48, tb)
            evac_f32(p, 48, gate_s[:, (tb - 4) * 512:(tb - 3) * 512], func=AF.Sigmoid)
        for c in range(8):
            s = load_w(O_LX + c * 128, 128)
            for tb in range(8):
                p = mm_fm(s, 128, tb)
                evac_f32(p, 128, lx_s[c, :, tb * 512:(tb + 1) * 512])
        for c in range(8):
            s = load_w(O_LG + c * 128, 128)
            for tb in range(4, 8):
                p = mm_fm(s, 128, tb)
                evac_f32(p, 128, lg_s[c, :, (tb - 4) * 512:(tb - 3) * 512])

        flush_pending()

        def load_w2(cols):
            s = cnt["w2"] % 2
            cnt["w2"] += 1
            off = 0
            for (c0, m) in cols:
                sc.dma(lambda e, c0=c0, m=m, off=off: e.dma_start(out=wf2[s][:, :, off:off + m], in_=w_in_v[:, :, c0:c0 + m]),
                       writes=[("wf2", s, off)])
                off += m
            sc.pool(lambda e: e.tensor_tensor(out=wb2[s][:, :, :], in0=wf2[s][:, :, :],
                                              in1=g1[:, :].unsqueeze(2).to_broadcast([128, 8, 512]), op=ALU.mult),
                    reads=[("wf2", s, o) for o in (0, 256)] + ["g1"], writes=[("wb2", s)])
            return s

        def mm_tm(s, t):
            p = cnt["ps"] % 4
            cnt["ps"] += 1
            for kc in range(8):
                sc.pe(lambda e, kc=kc: e.matmul(psA[:, p, :], lhsT=hT[:, kc, t * 128:(t + 1) * 128], rhs=wb2[s][:, kc, :],
                                                start=(kc == 0), stop=(kc == 7)),
                      reads=[("wb2", s), ("hT", t // 4)], writes=[("psA", p)])
            return p

        s = load_w2([(O_KV + 3 * 256, 256), (O_KV + 5 * 256, 256)])
        for t in range(NT):
            p = mm_tm(s, t)
            e3 = cnt["ev"] % 3
            cnt["ev"] += 1
            sc.act(lambda e, p=p, e3=e3: e.copy(out=evb[e3][:, :], in_=psA[:, p, :]), reads=[("psA", p)], writes=[("evb", e3)])
            sc.dma(lambda e, e3=e3, t=t: e.dma_start(out=v_s[:, t * 128:(t + 1) * 128, :].rearrange("s t c -> t s c"),
                                                     in_=evb[e3][:, :].rearrange("t (s c) -> t s c", s=2)),
                   reads=[("evb", e3)], q="gpsimd")
        for mc in range(4):
            s = load_w2([(O_MG + mc * 512, 256), (O_MG + mc * 512 + 256, 256)])
            for t in range(16, NT):
                p = mm_tm(s, t)
                evac_f32(p, 128, mg_s[(t - 16) * 128:(t - 15) * 128, mc * 512:(mc + 1) * 512], func=AF.Sigmoid)
        if upto == "B":
            sc.emit()
            pb.close()
            hst.close()
            st.close()
            return nc
        sc.emit()
    hst.close()


    with ExitStack() as pc:
        sbc = lambda name, shape, dt=F32: pc.enter_context(nc.sbuf_tensor("sc_" + name, list(shape), dt))
        ck = sbc("ck", [128, 4, T + 32], BF16)
        posf = sbc("posf", [64, 32])
        w1f = sbc("w1f", [64, 32, 256])
        w1b = sbc("w1b", [128, 32, 256], BF16)
        w2f = sbc("w2f", [128, 2, 64])
        w2b = sbc("w2b", [128, 2, 64], BF16)
        b1c = sbc("b1c", [128, 2])
        b2c = sbc("b2c", [64, 1])
        b2rf = sbc("b2rf", [1, 64])
        b2rb = sbc("b2rb", [1, 64], BF16)
        onesr = sbc("onesr", [1, 128], BF16)
        hidb = sbc("hidb", [128, 2, 4, 256], BF16)
        biasc = [sbc(f"biasc{i}", [128, 1]) for i in range(2)]
        gu = [sbc(f"gu{i}", [128, 256]) for i in range(2)]
        gt = [sbc(f"gt{i}", [128, 256]) for i in range(2)]
        kf = [sbc(f"kf{i}", [64, 256]) for i in range(2)]
        ksq = [sbc(f"ksq{i}", [64, 256]) for i in range(2)]
        krs = [sbc(f"krs{i}", [64, 256]) for i in range(2)]
        kob = [sbc(f"kob{i}", [128, 256], BF16) for i in range(2)]
        sc.dve(lambda e: e.memset(onesr[:, :], 1.0), writes=["onesr"])
        sc.pool(lambda e: e.memset(ck[64:128, :, :], 0.0), writes=["ckz"])
        sc.pool(lambda e: e.memset(w1b[64:128, :, :], 0.0), writes=["w1bz"])
        cc = {"ps": 0, "g": 0, "k": 0}
        for si in range(2):
            sc.dma(lambda e, si=si: e.dma_start(out=ck[0:64, :, 0:T], in_=ckT_s[si].rearrange("g d t -> d g t")), writes=["ck"])
            sc.dma(lambda e, si=si: e.dma_start(out=posf[:, :], in_=posT_in[si]), writes=["posf"])
            sc.act(lambda e: e.copy(out=ck[0:64, :, T:T + 32], in_=posf[:, :].unsqueeze(1).to_broadcast([64, 4, 32])),
                   reads=["posf"], writes=["ck"])
            sc.dma(lambda e, si=si: e.dma_start(out=w1f[:, :, :], in_=w1_in[si].rearrange("l d f -> d l f")), writes=["w1f"])
            sc.pool(lambda e: e.tensor_copy(out=w1b[0:64, :, :], in_=w1f[:, :, :]), reads=["w1f"], writes=["w1b"])
            sc.dma(lambda e, si=si: e.dma_start(out=w2f[:, :, :], in_=w2_in[si]), writes=["w2f"])
            sc.pool(lambda e: e.tensor_copy(out=w2b[:, :, :], in_=w2f[:, :, :]), reads=["w2f"], writes=["w2b"])
            sc.dma(lambda e, si=si: e.dma_start(out=b1c[:, :], in_=b1_in[si]), writes=["b1c"])
            sc.dma(lambda e, si=si: e.dma_start(out=b2c[:, :], in_=b2c_in[si]), writes=["b2c"])
            sc.dma(lambda e, si=si: e.dma_start(out=b2rf[:, :], in_=b2r_in[si]), writes=["b2rf"])
            sc.dve(lambda e: e.tensor_copy(out=b2rb[:, :], in_=b2rf[:, :]), reads=["b2rf"], writes=["b2rb"])
            for g in range(4):
                for fc in range(2):
                    p = cc["ps"] % 4
                    cc["ps"] += 1
                    for l in range(32):
                        sc.pe(lambda e, p=p, l=l, g=g, fc=fc: e.matmul(
                            psA[:, p, 0:257], lhsT=w1b[:, l, fc * 128:(fc + 1) * 128], rhs=ck[:, g, l:l + 4097:16],
                            start=(l == 0), stop=(l == 31)), reads=["w1b", "ck", "ckz", "w1bz"], writes=[("psA", p)])
                    i2 = cc["g"] % 2
                    cc["g"] += 1
                    sc.dve(lambda e, p=p, fc=fc, i2=i2: e.tensor_tensor(out=biasc[i2][:, :], in0=psA[:, p, 256:257],
                                                                        in1=b1c[:, fc:fc + 1], op=ALU.add),
                           reads=[("psA", p), "b1c"], writes=[("biasc", i2)])
                    sc.act(lambda e, p=p, i2=i2: e.activation(out=gu[i2][:, :], in_=psA[:, p, 0:256], func=AF.Identity,
                                                              bias=biasc[i2][:, 0:1]),
                           reads=[("psA", p), ("biasc", i2)], writes=[("gu", i2)])
                    sc.dve(lambda e, i2=i2: e.tensor_tensor(out=gt[i2][:, :], in0=gu[i2][:, :], in1=gu[i2][:, :], op=ALU.mult),
                           reads=[("gu", i2)], writes=[("gt", i2)])
                    sc.dve(lambda e, i2=i2: e.tensor_scalar(out=gt[i2][:, :], in0=gt[i2][:, :], scalar1=0.044715, scalar2=1.0,
                                                            op0=ALU.mult, op1=ALU.add),
                           reads=[("gt", i2)], writes=[("gt", i2)])
                    sc.dve(lambda e, i2=i2: e.tensor_tensor(out=gt[i2][:, :], in0=gt[i2][:, :], in1=gu[i2][:, :], op=ALU.mult),
                           reads=[("gt", i2), ("gu", i2)], writes=[("gt", i2)])
                    sc.act(lambda e, i2=i2: e.activation(out=gt[i2][:, :], in_=gt[i2][:, :], func=AF.Sigmoid, scale=1.5957691216),
                           reads=[("gt", i2)], writes=[("gt", i2)])
                    sc.dve(lambda e, i2=i2, fc=fc, g=g: e.tensor_tensor(out=hidb[:, fc, g, :], in0=gt[i2][:, :], in1=gu[i2][:, :],
                                                                        op=ALU.mult),
                           reads=[("gt", i2), ("gu", i2)], writes=[("hidb", g)])
                if si == 0:
                    p = cc["ps"] % 4
                    cc["ps"] += 1
                    for fc in range(2):
                        sc.pe(lambda e, p=p, fc=fc, g=g: e.matmul(psA[0:64, p, 0:256], lhsT=w2b[:, fc, :], rhs=hidb[:, fc, g, :],
                                                                  start=(fc == 0), stop=(fc == 1)),
                              reads=["w2b", ("hidb", g)], writes=[("psA", p)])
                    k2 = cc["k"] % 2
                    cc["k"] += 1
                    sc.act(lambda e, p=p, k2=k2: e.activation(out=kf[k2][:, :], in_=psA[0:64, p, 0:256], func=AF.Identity,
                                                              bias=b2c[:, 0:1]),
                           reads=[("psA", p), "b2c"], writes=[("kf", k2)])
                    sc.act(lambda e, k2=k2: e.activation(out=ksq[k2][:, :], in_=kf[k2][:, :], func=AF.Square),
                           reads=[("kf", k2)], writes=[("ksq", k2)])
                    sc.pe(lambda e, k2=k2: e.matmul(psA[0:64, 4 + k2, 0:256], lhsT=ones64[:, :], rhs=ksq[k2][:, :], start=True, stop=True),
                          reads=[("ksq", k2), "ones64"], writes=[("psA", 4 + k2)])
                    sc.dve(lambda e, k2=k2: e.tensor_scalar(out=krs[k2][:, :], in0=psA[0:64, 4 + k2, 0:256], scalar1=1.0 / 64,
                                                            scalar2=EPS, op0=ALU.mult, op1=ALU.add),
                           reads=[("psA", 4 + k2)], writes=[("krs", k2)])
                    sc.dve(lambda e, k2=k2: e.reciprocal(out=krs[k2][:, :], in_=krs[k2][:, :]), reads=[("krs", k2)], writes=[("krs", k2)])
                    sc.act(lambda e, k2=k2: e.activation(out=krs[k2][:, :], in_=krs[k2][:, :], func=AF.Sqrt),
                           reads=[("krs", k2)], writes=[("krs", k2)])
                    sc.dve(lambda e, k2=k2: e.scalar_tensor_tensor(out=kob[k2][0:64, :], in0=kf[k2][:, :], scalar=qg[:, 1:2],
                                                                   in1=krs[k2][:, :], op0=ALU.mult, op1=ALU.mult),
                           reads=[("kf", k2), ("krs", k2), "qg"], writes=[("kob", k2)])
                    sc.dma(lambda e, k2=k2, g=g: e.dma_start(out=kcT_s[g], in_=kob[k2][0:64, :]), reads=[("kob", k2)], q="gpsimd")
                else:
                    for nt in range(2):
                        p = cc["ps"] % 4
                        cc["ps"] += 1
                        for fc in range(2):
                            sc.pe(lambda e, p=p, fc=fc, g=g, nt=nt: e.matmul(
                                psA[:, p, 0:64], lhsT=hidb[:, fc, g, nt * 128:(nt + 1) * 128], rhs=w2b[:, fc, :],
                                start=(fc == 0), stop=False), reads=["w2b", ("hidb", g)], writes=[("psA", p)])
                        sc.pe(lambda e, p=p: e.matmul(psA[:, p, 0:64], lhsT=onesr[:, :], rhs=b2rb[:, :], start=False, stop=True),
                              reads=["onesr", "b2rb"], writes=[("psA", p)])
                        k2 = cc["k"] % 2
                        cc["k"] += 1
                        sc.act(lambda e, p=p, k2=k2: e.copy(out=kob[k2][:, 0:64], in_=psA[:, p, 0:64]),
                               reads=[("psA", p)], writes=[("kob", k2)])
                        sc.dma(lambda e, k2=k2, g=g, nt=nt: e.dma_start(out=vc_s[nt * 128:(nt + 1) * 128, g, :], in_=kob[k2][:, 0:64]),
                               reads=[("kob", k2)], q="gpsimd")
        if upto == "C":
            sc.emit()
            pc.close()
            st.close()
            return nc
        sc.emit()

    if "D" not in SKIP:
      with ExitStack() as pd:
        sbd = lambda name, shape, dt=F32: pd.enter_context(nc.sbuf_tensor("sd_" + name, list(shape), dt))
        lx = [sbd(f"lx{i}", [128, T + 3]) for i in range(2)]
        xc = sbd("xc", [128, T])
        rbuf = sbd("rbuf", [128, T])
        ibuf = sbd("ibuf", [128, T])
        tbuf = sbd("tbuf", [128, T])
        hbuf = sbd("hbuf", [128, T])
        lg = [sbd(f"lg{i}", [128, TO]) for i in range(2)]
        gl = sbd("gl", [128, TO])
        lob = [sbd(f"lob{i}", [128, TO], BF16) for i in range(2)]
        lcw = sbd("lcw", [128, 8, 4])
        lpar = sbd("lpar", [128, 4, 8])
        hf = sbd("hf", [128, 1])
        bd = [sbd(f"bd{i}", [128, 2, 128]) for i in range(2)]
        sp = sbd("sp", [128, 8])
        spx = sbd("spx", [128, 8])
        spl = sbd("spl", [128, 8])
        spt = sbd("spt", [128, 8])
        spm = sbd("spm", [128, 8])
        sc8 = sbd("sc8", [128, 8])
        sc.dma(lambda e: e.dma_start(out=lcw[:, :, :], in_=lcw_in), writes=["lcw"])
        sc.dma(lambda e: e.dma_start(out=lpar[:, :, :], in_=lpar_in), writes=["lpar"])
        sc.dma(lambda e: e.dma_start(out=hf[:, :], in_=hf_in), writes=["hf"])
        for i in range(2):
            sc.dve(lambda e, i=i: e.memset(lx[i][:, 0:3], 0.0), writes=[("lxz", i)])
        lam = lpar[:, 3, :]
        sc.dve(lambda e: e.tensor_scalar(out=spx[:, :], in0=lam, scalar1=-1.0, scalar2=None, op0=ALU.mult), reads=["lpar"], writes=["spx"])
        sc.dve(lambda e: e.tensor_tensor(out=spx[:, :], in0=spx[:, :], in1=lam, op=ALU.max), reads=["lpar", "spx"], writes=["spx"])
        sc.act(lambda e: e.activation(out=spx[:, :], in_=spx[:, :], func=AF.Exp, scale=-1.0), reads=["spx"], writes=["spx"])
        sc.act(lambda e: e.activation(out=spl[:, :], in_=spx[:, :], func=AF.Ln, bias=1.0), reads=["spx"], writes=["spl"])
        sc.dve(lambda e: e.tensor_scalar(out=spt[:, :], in0=spx[:, :], scalar1=-0.25, scalar2=1.0 / 3.0, op0=ALU.mult, op1=ALU.add),
               reads=["spx"], writes=["spt"])
        sc.dve(lambda e: e.tensor_tensor(out=spt[:, :], in0=spt[:, :], in1=spx[:, :], op=ALU.mult), reads=["spt", "spx"], writes=["spt"])
        sc.dve(lambda e: e.tensor_scalar(out=spt[:, :], in0=spt[:, :], scalar1=-1.0, scalar2=0.5, op0=ALU.mult, op1=ALU.add),
               reads=["spt"], writes=["spt"])
        sc.dve(lambda e: e.tensor_tensor(out=spt[:, :], in0=spt[:, :], in1=spx[:, :], op=ALU.mult), reads=["spt", "spx"], writes=["spt"])
        sc.dve(lambda e: e.tensor_scalar(out=spt[:, :], in0=spt[:, :], scalar1=-1.0, scalar2=1.0, op0=ALU.mult, op1=ALU.add),
               reads=["spt"], writes=["spt"])
        sc.dve(lambda e: e.tensor_tensor(out=spt[:, :], in0=spt[:, :], in1=spx[:, :], op=ALU.mult), reads=["spt", "spx"], writes=["spt"])
        sc.dve(lambda e: e.tensor_single_scalar(out=spm[:, :], in_=spx[:, :], scalar=0.03, op=ALU.is_lt), reads=["spx"], writes=["spm"])
        sc.dve(lambda e: e.tensor_tensor(out=spt[:, :], in0=spt[:, :], in1=spl[:, :], op=ALU.subtract), reads=["spt", "spl"], writes=["spt"])
        sc.dve(lambda e: e.tensor_tensor(out=spt[:, :], in0=spt[:, :], in1=spm[:, :], op=ALU.mult), reads=["spt", "spm"], writes=["spt"])
        sc.dve(lambda e: e.tensor_tensor(out=spl[:, :], in0=spl[:, :], in1=spt[:, :], op=ALU.add), reads=["spt", "spl"], writes=["spl"])
        sc.dve(lambda e: e.tensor_scalar(out=sp[:, :], in0=lam, scalar1=-1.0, scalar2=0.0, op0=ALU.mult, op1=ALU.max),
               reads=["lpar"], writes=["sp"])
        sc.dve(lambda e: e.tensor_tensor(out=sp[:, :], in0=sp[:, :], in1=spl[:, :], op=ALU.add), reads=["sp", "spl"], writes=["sp"])
        sc.dve(lambda e: e.tensor_scalar(out=sc8[:, :], in0=sp[:, :], scalar1=-8.0, scalar2=None, op0=ALU.mult), reads=["sp"], writes=["sc8"])
        cd = {"ps": 0}
        for c in range(8):
            li = c % 2
            sc.dma(lambda e, li=li, c=c: e.dma_start(out=lx[li][:, 3:3 + T], in_=lx_s[c]), writes=[("lx", li)])
            sc.dma(lambda e, li=li, c=c: e.dma_start(out=bd[li][:, :, :], in_=lbd_in[:, c, :, :].rearrange("w k j -> k w j")),
                   writes=[("bd", li)])
            sc.dma(lambda e, li=li, c=c: e.dma_start(out=lg[li][:, :], in_=lg_s[c]), writes=[("lg", li)])
            sc.dve(lambda e, li=li, c=c: e.tensor_scalar(out=xc[:, :], in0=lx[li][:, 0:T], scalar1=lcw[:, c, 0:1], scalar2=lpar[:, 0, c:c + 1],
                                                         op0=ALU.mult, op1=ALU.add),
                   reads=[("lx", li), ("lxz", li), "lcw", "lpar"], writes=["xc"])
            for j in range(1, 4):
                sc.dve(lambda e, li=li, c=c, j=j: e.scalar_tensor_tensor(out=xc[:, :], in0=lx[li][:, j:j + T], scalar=lcw[:, c, j:j + 1],
                                                                         in1=xc[:, :], op0=ALU.mult, op1=ALU.add),
                       reads=[("lx", li), ("lxz", li), "lcw", "xc"], writes=["xc"])
            for (w, dstb, nm, prow) in ((0, rbuf, "rbuf", 1), (1, ibuf, "ibuf", 2)):
                for tb in range(8):
                    p = cd["ps"] % 4
                    cd["ps"] += 1
                    sc.pe(lambda e, p=p, li=li, w=w, tb=tb: e.matmul(psA[:, p, :], lhsT=bd[li][:, w, :], rhs=xc[:, tb * 512:(tb + 1) * 512],
                                                                     start=True, stop=True),
                          reads=[("bd", li), "xc"], writes=[("psA", p)])
                    sc.act(lambda e, p=p, dstb=dstb, tb=tb, prow=prow, c=c: e.activation(
                        out=dstb[:, tb * 512:(tb + 1) * 512], in_=psA[:, p, :], func=AF.Sigmoid, bias=lpar[:, prow, c:c + 1]),
                        reads=[("psA", p), "lpar"], writes=[(nm, tb)])
            RB = [("rbuf", tb) for tb in range(8)]
            IB = [("ibuf", tb) for tb in range(8)]
            sc.act(lambda e, c=c: e.activation(out=rbuf[:, :], in_=rbuf[:, :], func=AF.Exp, scale=sc8[:, c:c + 1]),
                   reads=RB + ["sc8"], writes=RB)
            sc.dve(lambda e: e.tensor_tensor(out=tbuf[:, :], in0=rbuf[:, :], in1=rbuf[:, :], op=ALU.mult), reads=RB, writes=["tbuf"])
            sc.act(lambda e: e.activation(out=tbuf[:, :], in_=tbuf[:, :], func=AF.Sqrt, scale=-1.0, bias=1.0), reads=["tbuf"], writes=["tbuf"])
            sc.dve(lambda e: e.tensor_tensor(out=ibuf[:, :], in0=ibuf[:, :], in1=xc[:, :], op=ALU.mult), reads=IB + ["xc"], writes=IB)
            sc.dve(lambda e: e.scalar_tensor_tensor(out=ibuf[:, 0:TO], in0=tbuf[:, 0:TO], scalar=hf[:, 0:1], in1=ibuf[:, 0:TO],
                                                    op0=ALU.mult, op1=ALU.mult), reads=IB + ["tbuf", "hf"], writes=IB)
            sc.dve(lambda e: e.tensor_tensor(out=ibuf[:, TO:T], in0=tbuf[:, TO:T], in1=ibuf[:, TO:T], op=ALU.mult),
                   reads=IB + ["tbuf"], writes=IB)
            sc.dve(lambda e: e.tensor_tensor_scan(out=hbuf[:, :], data0=rbuf[:, :], data1=ibuf[:, :], initial=0.0,
                                                  op0=ALU.mult, op1=ALU.add), reads=RB + IB, writes=["hbuf"])
            sc.pool(lambda e, li=li: e.tensor_tensor(out=gl[:, :], in0=lg[li][:, :], in1=lg[li][:, :], op=ALU.mult),
                   reads=[("lg", li)], writes=["gl"])
            sc.dve(lambda e: e.tensor_scalar(out=gl[:, :], in0=gl[:, :], scalar1=0.044715, scalar2=1.0, op0=ALU.mult, op1=ALU.add),
                   reads=["gl"], writes=["gl"])
            sc.pool(lambda e, li=li: e.tensor_tensor(out=gl[:, :], in0=gl[:, :], in1=lg[li][:, :], op=ALU.mult),
                   reads=["gl", ("lg", li)], writes=["gl"])
            sc.act(lambda e: e.activation(out=gl[:, :], in_=gl[:, :], func=AF.Sigmoid, scale=1.5957691216), reads=["gl"], writes=["gl"])
            sc.pool(lambda e, li=li: e.tensor_tensor(out=gl[:, :], in0=gl[:, :], in1=lg[li][:, :], op=ALU.mult),
                   reads=["gl", ("lg", li)], writes=["gl"])
            sc.dve(lambda e, li=li: e.tensor_tensor(out=lob[li][:, :], in0=gl[:, :], in1=hbuf[:, TO:T], op=ALU.mult),
                   reads=["gl", "hbuf"], writes=[("lob", li)])
            sc.dma(lambda e, li=li, c=c: e.dma_start(out=lruT_s[c], in_=lob[li][:, :]), reads=[("lob", li)], q="gpsimd")
        if upto == "D":
            sc.emit()
            pd.close()
            st.close()
            return nc
        sc.emit()

    with ExitStack() as pe_:
        sbe = lambda name, shape, dt=F32: pe_.enter_context(nc.sbuf_tensor("se_" + name, list(shape), dt))
        kS = sbe("kS", [128, 4, T], BF16)
        kW = sbe("kW", [128, 4, T], BF16)
        vS = sbe("vS", [128, NT, 4, 65], BF16)
        vW = sbe("vW", [128, NT, 4, 65], BF16)
        kC = sbe("kC", [128, 4, 256], BF16)
        vC = sbe("vC", [128, 2, 4, 65], BF16)
        esel = sbe("esel", [128, 32, 128], BF16)
        cmask = sbe("cmask", [128, 16, 2, 128], BF16)
        mfar = sbe("mfar", [128, 128], BF16)
        mdiag = sbe("mdiag", [128, 128], BF16)
        bmap = sbe("bmap", [128, 2, 65], BF16)
        bonus = sbe("bonus", [128, 16, 64])
        hb = sbe("hb", [128, 1])
        ones65 = sbe("ones65", [65, 64])
        qp = [sbe(f"qp{i}", [128, 4, 128], BF16) for i in range(2)]
        gsb = [sbe(f"gsb{i}", [65, 3, 4, 128]) for i in range(2)]
        pt = [sbe(f"pt{i}", [128, 512], BF16) for i in range(4)]
        impb = sbe("impb", [128, 64])
        imp2 = sbe("imp2", [128, 64])
        rz = sbe("rz", [128, 4])
        m8a = sbe("m8a", [128, 8])
        m8b = sbe("m8b", [128, 8])
        mnq = sbe("mnq", [128, 64], BF16)
        mneg = [sbe(f"mneg{i}", [128, 4, 128], BF16) for i in range(2)]
        wrow = [sbe(f"wrow{i}", [65, 512]) for i in range(4)]
        wrowb = [sbe(f"wrowb{i}", [65, 512], BF16) for i in range(4)]
        ones65b = sbe("ones65b", [65, 64], BF16)
        bcs = [sbe(f"bcs{i}", [64, 512]) for i in range(2)]
        tmpf = [sbe(f"tmpf{i}", [64, 512]) for i in range(2)]
        accf = [sbe(f"accf{i}", [64, 512]) for i in range(2)]
        aob = [sbe(f"aob{i}", [64, 4, 128], BF16) for i in range(2)]

        sc.pool(lambda e: e.memset(kS[64:128, :, :], 0.0), writes=[("kS", "z")])
        sc.pool(lambda e: e.memset(kW[64:128, :, :], 0.0), writes=[("kW", "z")])
        sc.pool(lambda e: e.memset(kC[64:128, :, :], 0.0), writes=[("kC", "z")])
        for i in range(2):
            sc.dve(lambda e, i=i: e.memset(qp[i][64:128, :, :], 0.0), writes=[("qp", i, "z")])
            sc.dve(lambda e, i=i: e.memset(mneg[i][64:128, :, :], 0.0), writes=[("mneg", i, "z")])
        for (kbuf, si) in ((kS, 0), (kW, 1)):
            nm = "kS" if si == 0 else "kW"
            sc.dma(lambda e, kbuf=kbuf, si=si: e.dma_start(out=kbuf[0:64, :, :], in_=kT_s[si].rearrange("g d t -> d g t")),
                   writes=[(nm, "a")])
            for g in range(4):
                sc.dma(lambda e, kbuf=kbuf, g=g: e.dma_start(out=kbuf[64:71, g, :], in_=kx_in), reads=[(nm, "z")], writes=[(nm, "x", g)])
        for (vbuf, si) in ((vS, 0), (vW, 1)):
            nm = "vS" if si == 0 else "vW"
            sc.pool(lambda e, vbuf=vbuf: e.memset(vbuf[:, :, :, 64:65], 1.0), writes=[(nm, "o")])
            for t in range(NT):
                sc.dma(lambda e, vbuf=vbuf, si=si, t=t: e.dma_start(
                    out=vbuf[:, t, :, 0:64], in_=v_s[si, t * 128:(t + 1) * 128, :].rearrange("p (g d) -> p g d", g=4)),
                    writes=[(nm, t // 4)])
        sc.dma(lambda e: e.dma_start(out=kC[0:64, :, :], in_=kcT_s.rearrange("g d n -> d g n")), writes=[("kC", "a")])
        for g in range(4):
            sc.dma(lambda e, g=g: e.dma_start(out=kC[64:71, g, :], in_=kcx_in), reads=[("kC", "z")], writes=[("kC", "x", g)])
        sc.pool(lambda e: e.memset(vC[:, :, :, 64:65], 1.0), writes=[("vC", "o")])
        for nt in range(2):
            sc.dma(lambda e, nt=nt: e.dma_start(out=vC[:, nt, :, 0:64], in_=vc_s[nt * 128:(nt + 1) * 128, :, :]), writes=[("vC", "a", nt)])
        for (dst, src, nm) in ((esel, esel_in, "esel"), (cmask, cmask_in, "cmask"), (mfar, mfar_in, "mfar"), (mdiag, mdiag_in, "mdiag"),
                               (bmap, bmap_in, "bmap"), (bonus, bonus_in, "bonus"), (hb, hb_in, "hb")):
            if nm == "cmask":
                sc.dma(lambda e, dst=dst, src=src: e.dma_start(out=dst[:, :, :, :].rearrange("p a b c -> p (a b c)"),
                                                               in_=src.rearrange("p a b c -> p (a b c)")), writes=[nm])
            else:
                sc.dma(lambda e, dst=dst, src=src: e.dma_start(out=dst[tuple(slice(None) for _ in dst.shape)], in_=src), writes=[nm])
        sc.dve(lambda e: e.memset(ones65[:, :], 1.0), writes=["ones65"])
        sc.dve(lambda e: e.memset(ones65b[:, :], 1.0), writes=["ones65b"])
        KS_ALL = [("kS", "z"), ("kS", "a")] + [("kS", "x", g) for g in range(4)]
        KW_ALL = [("kW", "z"), ("kW", "a")] + [("kW", "x", g) for g in range(4)]
        KC_ALL = [("kC", "z"), ("kC", "a")] + [("kC", "x", g) for g in range(4)]
        VS_ALL = [("vS", "o")] + [("vS", t4) for t4 in range(8)]
        VW_ALL = [("vW", "o")] + [("vW", t4) for t4 in range(8)]
        VC_ALL = [("vC", "o"), ("vC", "a", 0), ("vC", "a", 1)]
        ce = {"it": 0, "s": 0, "pt": 0, "o": 0, "f": 0}

        def score_tile(lhsT, lhs_keys, qi, masked_add=None, bias=None, mask=None, mask_keys=(), rows=128):
            sl = ce["s"] % 3
            ce["s"] += 1
            pi = ce["pt"] % 4
            ce["pt"] += 1
            sc.pe(lambda e: e.matmul(psA[0:rows, sl, :], lhsT=lhsT, rhs=qp[qi][:, :, :], start=True, stop=(masked_add is None)),
                  reads=list(lhs_keys) + [("qp", qi, "a"), ("qp", qi, "x"), ("qp", qi, "z")], writes=[("psA", sl)])
            if masked_add is not None:
                kt, mi = masked_add
                sc.pe(lambda e: e.matmul(psA[:, sl, :], lhsT=esel[:, kt, :], rhs=mneg[mi][:, :, :], start=False, stop=True),
                      reads=["esel", ("mneg", mi), ("mneg", mi, "z")], writes=[("psA", sl)])
            if bias is None:
                sc.act(lambda e: e.activation(out=pt[pi][0:rows, :], in_=psA[0:rows, sl, :], func=AF.Exp, scale=0.125),
                       reads=[("psA", sl)], writes=[("pt", pi)])
            else:
                sc.act(lambda e: e.activation(out=pt[pi][0:rows, :], in_=psA[0:rows, sl, :], func=AF.Exp, scale=0.125, bias=hb[0:rows, 0:1]),
                       reads=[("psA", sl), "hb"], writes=[("pt", pi)])
            if mask is not None:
                sc.dve(lambda e: e.tensor_tensor(out=pt[pi][0:rows, :].rearrange("k (h q) -> k h q", h=4),
                                                 in0=pt[pi][0:rows, :].rearrange("k (h q) -> k h q", h=4),
                                                 in1=mask.unsqueeze(1).to_broadcast([rows, 4, 128]), op=ALU.mult),
                       reads=[("pt", pi)] + list(mask_keys), writes=[("pt", pi)])
            return pi

        hooks = {}

        def finalize(ob, x, qi, fi, first, last, g, qt, step):
            w4 = ce["f"] % 4
            w = ce["f"] % 2
            ce["f"] += 1
            sc.dve(lambda e: e.tensor_scalar_max(out=wrow[w4][64:65, :], in0=psA[64:65, ob, :], scalar1=1e-30),
                   reads=[("psA", ob)], writes=[("wrow", w4)])
            sc.dve(lambda e: e.reciprocal(out=wrow[w4][64:65, :], in_=wrow[w4][64:65, :]), reads=[("wrow", w4)], writes=[("wrow", w4)])
            sc.dve(lambda e: e.tensor_tensor(out=wrowb[w4][64:65, :], in0=wrow[w4][64:65, :],
                                             in1=gsb[qi][64:65, x, :, :].rearrange("p h q -> p (h q)"), op=ALU.mult),
                   reads=[("wrow", w4), ("gsb", qi, x)], writes=[("wrowb", w4)])

            def part2():
                bsl = ce["s"] % 3
                ce["s"] += 1
                sc.pe(lambda e: e.matmul(psA[0:64, bsl, :], lhsT=ones65b[64:65, :], rhs=wrowb[w4][64:65, :], start=True, stop=True),
                      reads=[("wrowb", w4), "ones65b"], writes=[("psA", bsl)])
                sc.act(lambda e: e.copy(out=bcs[w][:, :], in_=psA[0:64, bsl, :]), reads=[("psA", bsl)], writes=[("bcs", w)])
                if first:
                    sc.dve(lambda e: e.tensor_tensor(out=accf[fi][:, :], in0=psA[0:64, ob, :], in1=bcs[w][:, :], op=ALU.mult),
                           reads=[("psA", ob), ("bcs", w)], writes=[("accf", fi)])
                else:
                    sc.dve(lambda e: e.tensor_tensor(out=tmpf[w][:, :], in0=psA[0:64, ob, :], in1=bcs[w][:, :], op=ALU.mult),
                           reads=[("psA", ob), ("bcs", w)], writes=[("tmpf", w)])
                    if not last:
                        sc.pool(lambda e: e.tensor_tensor(out=accf[fi][:, :], in0=accf[fi][:, :], in1=tmpf[w][:, :], op=ALU.add),
                                reads=[("accf", fi), ("tmpf", w)], writes=[("accf", fi)])
                    else:
                        sc.pool(lambda e: e.tensor_tensor(out=aob[fi][:, :, :].rearrange("d h q -> d (h q)"), in0=accf[fi][:, :],
                                                          in1=tmpf[w][:, :], op=ALU.add),
                                reads=[("accf", fi), ("tmpf", w)], writes=[("aob", fi)])
                        sc.dma(lambda e: e.dma_start(out=attnT_s[g * 4:(g + 1) * 4, :, qt * 128:(qt + 1) * 128].rearrange("h d q -> d h q"),
                                                     in_=aob[fi][:, :, :]), reads=[("aob", fi)], q="gpsimd")
            hooks.setdefault(step + DEFER[0 if first else (2 if last else 1)], []).append(part2)

        jobs = []

        jl = {}

        def add_job(s1, s2, it, br):
            jl.setdefault((it, br), []).append((s1, s2))

        for qt in range(QT_LIMIT if QT_LIMIT else 16):
            vt = 16 + qt
            for g in range(4):
                it = ce["it"]
                ce["it"] += 1
                qi = it % 2
                fi = it % 2
                mi = it % 2
                crows = [128, 8 * (vt + 1) - 128]
                st_ = {"pis": [None, None]}

                def loads(qi=qi, g=g, qt=qt):
                    sc.dma(lambda e: e.dma_start(
                        out=qp[qi][0:64, :, :], in_=qT_s[g * 4:(g + 1) * 4, :, qt * 128:(qt + 1) * 128].rearrange("h d q -> d h q")),
                        writes=[("qp", qi, "a")])
                    sc.dma(lambda e: e.dma_start(
                        out=qp[qi][64:71, :, :], in_=qx_in[g * 4:(g + 1) * 4, :, qt * 128:(qt + 1) * 128].rearrange("h r q -> r h q")),
                        reads=[("qp", qi, "z")], writes=[("qp", qi, "x")])
                    for x in range(3):
                        sc.dma(lambda e, x=x: e.dma_start(
                            out=gsb[qi][64:65, x, :, :],
                            in_=gate_s[g * 12:(g + 1) * 12, qt * 128:(qt + 1) * 128].rearrange("(o h x) q -> o x h q", o=1, x=3)[:, x, :, :]),
                            writes=[("gsb", qi, x)])

                ob_c = 3 + ce["o"] % 3
                ce["o"] += 1
                for nt in range(2):
                    def s1(nt=nt, g=g, qi=qi, qt=qt, st_=st_, crows=crows, loads=loads):
                        if nt == 0:
                            loads()
                        rw = crows[nt]
                        st_["pis"][nt] = score_tile(kC[:, g, nt * 128:nt * 128 + rw], KC_ALL, qi, bias=(True if nt == 0 else None),
                                                    mask=cmask[0:rw, qt, nt, :], mask_keys=["cmask"], rows=rw)

                    def s2(step, nt=nt, g=g, qi=qi, qt=qt, fi=fi, mi=mi, st_=st_, crows=crows, ob=ob_c):
                        rw = crows[nt]
                        pi = st_["pis"][nt]
                        sc.pe(lambda e: e.matmul(psA[0:65, ob, :], lhsT=vC[0:rw, nt, g, :], rhs=pt[pi][0:rw, :],
                                                 start=(nt == 0), stop=(nt == 1)),
                              reads=VC_ALL + [("pt", pi)], writes=[("psA", ob)])
                        if nt == 0:
                            return
                        pis = st_["pis"]
                        for h in range(4):
                            for n2 in range(2):
                                r2 = crows[n2]
                                sc.pe(lambda e, h=h, n2=n2, r2=r2: e.matmul(psT_f[:, h, :], lhsT=pt[pis[n2]][0:r2, h * 128:(h + 1) * 128],
                                                                            rhs=bmap[0:r2, n2, :], start=(n2 == 0), stop=(n2 == 1)),
                                      reads=[("pt", pis[n2]), "bmap"], writes=["psR"])
                        sc.dve(lambda e: e.tensor_scalar_max(out=rz[:, :], in0=psT_f[:, :, 64], scalar1=1e-30), reads=["psR"], writes=["rz"])
                        sc.dve(lambda e: e.reciprocal(out=rz[:, :], in_=rz[:, :]), reads=["rz"], writes=["rz"])
                        for h in range(4):
                            sc.dve(lambda e, h=h: e.scalar_tensor_tensor(
                                out=impb[:, :], in0=psT_f[:, h, 0:64], scalar=rz[:, h:h + 1],
                                in1=(bonus[:, qt, :] if h == 0 else impb[:, :]), op0=ALU.mult, op1=ALU.add),
                                reads=["psR", "rz", "bonus", "impb"], writes=["impb"])
                        sc.dve(lambda e: e.max(out=m8a[:, :], in_=impb[:, :]), reads=["impb"], writes=["m8a"])
                        sc.dve(lambda e: e.match_replace(out=imp2[:, :], in_to_replace=m8a[:, :], in_values=impb[:, :], imm_value=-3.0e38),
                               reads=["impb", "m8a"], writes=["imp2"])
                        sc.dve(lambda e: e.max(out=m8b[:, :], in_=imp2[:, :]), reads=["imp2"], writes=["m8b"])
                        sc.dve(lambda e: e.tensor_scalar_max(out=m8b[:, 7:8], in0=m8b[:, 7:8], scalar1=-1.0e29), reads=["m8b"], writes=["m8b"])
                        sc.dve(lambda e: e.tensor_scalar(out=mnq[:, :], in0=impb[:, :], scalar1=m8b[:, 7:8], scalar2=-4096.0,
                                                         op0=ALU.is_lt, op1=ALU.mult), reads=["impb", "m8b"], writes=["mnq"])
                        def tpose():
                            sc.pe(lambda e: e.transpose(out=psT[0:64, 0, 0:128], in_=mnq[:, :], identity=ident[:, :]),
                                  reads=["mnq", "ident"], writes=["psTm"])
                            sc.act(lambda e: e.copy(out=mneg[mi][0:64, :, :], in_=psT[0:64, 0, 0:128].unsqueeze(1).to_broadcast([64, 4, 128])),
                                   reads=["psTm", ("mneg", mi, "z")], writes=[("mneg", mi)])
                        hooks.setdefault(step + 4, []).append(tpose)
                        finalize(ob, 0, qi, fi, True, False, g, qt, step)
                    add_job(s1, s2, it, "c")
                ob_w = 3 + ce["o"] % 3
                ce["o"] += 1
                for r in range(5):
                    kt = vt - 4 + r
                    st_w = {}

                    def s1(r=r, kt=kt, g=g, qi=qi, st_w=st_w):
                        mask, mk = (mfar[:, :], ["mfar"]) if r == 0 else ((mdiag[:, :], ["mdiag"]) if r == 4 else (None, ()))
                        st_w["pi"] = score_tile(kW[:, g, kt * 128:(kt + 1) * 128], KW_ALL, qi, bias=(True if kt < 16 else None),
                                                mask=mask, mask_keys=mk)

                    def s2(step, r=r, kt=kt, g=g, qi=qi, qt=qt, fi=fi, st_w=st_w, ob=ob_w):
                        pi = st_w["pi"]
                        sc.pe(lambda e: e.matmul(psA[0:65, ob, :], lhsT=vW[:, kt, g, :], rhs=pt[pi][:, :], start=(r == 0), stop=(r == 4)),
                              reads=VW_ALL + [("pt", pi)], writes=[("psA", ob)])
                        if r == 4:
                            finalize(ob, 2, qi, fi, False, False, g, qt, step)
                    add_job(s1, s2, it, "w")
                ob_s = 3 + ce["o"] % 3
                ce["o"] += 1
                for kt in range(vt + 1):
                    st_s = {}

                    def s1(kt=kt, vt=vt, g=g, qi=qi, mi=mi, st_s=st_s):
                        mask, mk = (mdiag[:, :], ["mdiag"]) if kt == vt else (None, ())
                        st_s["pi"] = score_tile(kS[:, g, kt * 128:(kt + 1) * 128], KS_ALL, qi, masked_add=(kt, mi), mask=mask, mask_keys=mk)

                    def s2(step, kt=kt, vt=vt, g=g, qi=qi, qt=qt, fi=fi, st_s=st_s, ob=ob_s):
                        pi = st_s["pi"]
                        sc.pe(lambda e: e.matmul(psA[0:65, ob, :], lhsT=vS[:, kt, g, :], rhs=pt[pi][:, :], start=(kt == 0), stop=(kt == vt)),
                              reads=VS_ALL + [("pt", pi)], writes=[("psA", ob)])
                        if kt == vt:
                            finalize(ob, 1, qi, fi, False, True, g, qt, step)
                    add_job(s1, s2, it, "s")
        n_it = ce["it"]
        KH = 12
        jobs = list(jl[(0, "c")]) + list(jl[(0, "w")])
        for it in range(n_it):
            sj = jl[(it, "s")]
            jobs += sj[:-KH]
            if it + 1 < n_it:
                jobs += jl[(it + 1, "c")]
            jobs += sj[-KH:]
            if it + 1 < n_it:
                jobs += jl[(it + 1, "w")]
        LA = 2
        for i in range(len(jobs) + LA + 8):
            for hk in hooks.get(i, ()):
                hk()
            if i < len(jobs):
                jobs[i][0]()
            if 0 <= i - LA < len(jobs):
                jobs[i - LA][1](i)
        if upto == "E":
            sc.emit()
            pe_.close()
            st.close()
            return nc
        sc.emit()

    with ExitStack() as pf:
        sbf = lambda name, shape, dt=F32: pf.enter_context(nc.sbuf_tensor("sf_" + name, list(shape), dt))
        h2T = sbf("h2T", [128, 8, TO], BF16)
        g2 = sbf("g2", [128, 8])
        sc.dma(lambda e: e.dma_start(out=g2[:, :], in_=g2_in), writes=["g2"])
        with ExitStack() as pfa:
            sfa = lambda name, shape, dt=F32: pfa.enter_context(nc.sbuf_tensor("sfa_" + name, list(shape), dt))
            wst = [sfa(f"wst{i}", [128, 8, 512]) for i in range(2)]
            Wb = [sfa(f"Wb{i}", [128, 8, D], BF16) for i in range(3)]
            aT = [sfa(f"aT{i}", [128, 8, 128], BF16) for i in range(2)]
            lT = [sfa(f"lT{i}", [128, 8, 128], BF16) for i in range(2)]
            mgt = [sfa(f"mgt{i}", [128, 2048]) for i in range(2)]
            xt2 = [sfa(f"xt2{i}", [128, D]) for i in range(2)]
            m1 = sfa("m1", [128, D])
            m2 = sfa("m2", [128, D])
            mbs = [sfa(f"mb{i}", [128, D], BF16) for i in range(2)]
            mT = sfa("mT", [128, 8, 128], BF16)
            x1t = [sfa(f"x1t{i}", [128, D]) for i in range(2)]
            sq2 = sfa("sq2", [128, D])
            ss2 = sfa("ss2", [128, 1])
            xn2 = sfa("xn2", [128, D], BF16)
            cf = {"w": 0}
            for wi, src in enumerate((woa_in, wol_in, wout_in)):
                srcv = src.rearrange("(kc p) n -> p kc n", p=128)
                for nb in range(2):
                    i = cf["w"] % 2
                    cf["w"] += 1
                    sc.dma(lambda e, i=i, srcv=srcv, nb=nb: e.dma_start(out=wst[i][:, :, :], in_=srcv[:, :, nb * 512:(nb + 1) * 512]),
                           writes=[("wst", i)])
                    sc.pool(lambda e, i=i, wi=wi, nb=nb: e.tensor_copy(out=Wb[wi][:, :, nb * 512:(nb + 1) * 512], in_=wst[i][:, :, :]),
                            reads=[("wst", i)], writes=[("Wb", wi, nb)])
            attn_v = attnT_s.rearrange("h d q -> (h d) q").rearrange("(kc p) q -> p kc q", p=128)
            def stage_x(t):
                i = t % 2
                sc.dma(lambda e, i=i, t=t: e.dma_start(out=aT[i][:, :, :], in_=attn_v[:, :, t * 128:(t + 1) * 128]), writes=[("aT", i)])
                sc.dma(lambda e, i=i, t=t: e.dma_start(out=lT[i][:, :, :], in_=lruT_s[:, :, t * 128:(t + 1) * 128].rearrange("c p q -> p c q")),
                       writes=[("lT", i)])
                sc.dma(lambda e, i=i, t=t: e.dma_start(out=mgt[i][:, :], in_=mg_s[t * 128:(t + 1) * 128, :]), writes=[("mgt", i)])
                sc.dma(lambda e, i=i, t=t: e.dma_start(out=xt2[i][:, :], in_=xv[TO + t * 128:TO + (t + 1) * 128, :]), writes=[("xt2", i)])
                for (wi, src_t, nm, base) in ((0, aT, "aT", 0), (1, lT, "lT", 2)):
                    for nb in range(2):
                        for kc in range(8):
                            sc.pe(lambda e, wi=wi, src_t=src_t, nb=nb, kc=kc, base=base, i=i: e.matmul(
                                psA[:, base + nb, :], lhsT=src_t[i][:, kc, :], rhs=Wb[wi][:, kc, nb * 512:(nb + 1) * 512],
                                start=(kc == 0), stop=(kc == 7)),
                                reads=[(nm, i), ("Wb", wi, nb)], writes=[("psA", base + nb)])
                sc.dve(lambda e, i=i: e.tensor_tensor(out=m1[:, :].rearrange("p (a b) -> p a b", a=2), in0=psA[:, 0:2, :],
                                                      in1=mgt[i][:, 0:D].rearrange("p (a b) -> p a b", a=2), op=ALU.mult),
                       reads=[("psA", 0), ("psA", 1), ("mgt", i)], writes=["m1"])
                sc.dve(lambda e, i=i: e.tensor_tensor(out=m2[:, :].rearrange("p (a b) -> p a b", a=2), in0=psA[:, 2:4, :],
                                                      in1=mgt[i][:, D:2 * D].rearrange("p (a b) -> p a b", a=2), op=ALU.mult),
                       reads=[("psA", 2), ("psA", 3), ("mgt", i)], writes=["m2"])
                sc.dve(lambda e, i=i: e.tensor_tensor(out=mbs[i][:, :], in0=m1[:, :], in1=m2[:, :], op=ALU.add), reads=["m1", "m2"], writes=[("mb", i)])

            def stage_y(t):
                i = t % 2
                for kc in range(8):
                    sc.pe(lambda e, kc=kc, i=i: e.transpose(out=psT[:, 0, kc * 128:(kc + 1) * 128], in_=mbs[i][:, kc * 128:(kc + 1) * 128],
                                                       identity=ident[:, :]), reads=[("mb", i), "ident"], writes=[("psT", 0)])
                sc.act(lambda e: e.copy(out=mT[:, :, :], in_=psT[:, 0, :].rearrange("p (k n) -> p k n", k=8)),
                       reads=[("psT", 0)], writes=["mT"])
                for nb in range(2):
                    for kc in range(8):
                        sc.pe(lambda e, nb=nb, kc=kc: e.matmul(psA[:, 4 + nb, :], lhsT=mT[:, kc, :], rhs=Wb[2][:, kc, nb * 512:(nb + 1) * 512],
                                                               start=(kc == 0), stop=(kc == 7)),
                              reads=["mT", ("Wb", 2, nb)], writes=[("psA", 4 + nb)])
                sc.dve(lambda e, i=i: e.tensor_tensor(out=x1t[i][:, :].rearrange("p (a b) -> p a b", a=2), in0=psA[:, 4:6, :],
                                                      in1=xt2[i][:, :].rearrange("p (a b) -> p a b", a=2), op=ALU.add),
                       reads=[("psA", 4), ("psA", 5), ("xt2", i)], writes=[("x1t", i)])
                sc.dma(lambda e, i=i, t=t: e.dma_start(out=x1_s[t * 128:(t + 1) * 128, :], in_=x1t[i][:, :]), reads=[("x1t", i)], q="gpsimd",
                       writes=[("x1_s", t)])
                sc.dve(lambda e, i=i: e.tensor_tensor(out=sq2[:, :], in0=x1t[i][:, :], in1=x1t[i][:, :], op=ALU.mult),
                       reads=[("x1t", i)], writes=["sq2"])
                sc.dve(lambda e: e.tensor_reduce(out=ss2[:, :], in_=sq2[:, :], axis=AX.X, op=ALU.add), reads=["sq2"], writes=["ss2"])
                sc.dve(lambda e: e.tensor_scalar(out=ss2[:, :], in0=ss2[:, :], scalar1=1.0 / D, scalar2=EPS, op0=ALU.mult, op1=ALU.add),
                       reads=["ss2"], writes=["ss2"])
                sc.dve(lambda e: e.reciprocal(out=ss2[:, :], in_=ss2[:, :]), reads=["ss2"], writes=["ss2"])
                sc.act(lambda e: e.activation(out=ss2[:, :], in_=ss2[:, :], func=AF.Sqrt), reads=["ss2"], writes=["ss2"])
                sc.act(lambda e, i=i: e.mul(out=xn2[:, :], in_=x1t[i][:, :], mul=ss2[:, 0:1]), reads=[("x1t", i), "ss2"], writes=["xn2"])
                for kc in range(8):
                    sc.pe(lambda e, kc=kc: e.transpose(out=psT[:, 0, kc * 128:(kc + 1) * 128], in_=xn2[:, kc * 128:(kc + 1) * 128],
                                                       identity=ident[:, :]), reads=["xn2", "ident"], writes=[("psT", 0)])
                sc.dve(lambda e, t=t: e.tensor_copy(out=h2T[:, :, t * 128:(t + 1) * 128], in_=psT[:, 0, :].rearrange("p (k n) -> p k n", k=8)),
                       reads=[("psT", 0)], writes=[("h2T", t // 4)])
            for t in range(17):
                if t < 16:
                    stage_x(t)
                if t >= 1:
                    stage_y(t - 1)
            if upto == "Fa":
                sc.emit()
                pfa.close()
                pf.close()
                st.close()
                return nc
        with ExitStack() as pfb:
            sfb = lambda name, shape, dt=F32: pfb.enter_context(nc.sbuf_tensor("sfb_" + name, list(shape), dt))
            wst = [sfb(f"wst{i}", [128, 8, 512]) for i in range(2)]
            Wd = sfb("Wd", [128, 22, D], BF16)
            actT = sfb("actT", [128, 22, 1024], BF16)
            wgf = [sfb(f"wgf{i}", [128, 2, 8, 128]) for i in range(2)]
            wgb = [sfb(f"wgb{i}", [128, 2, 8, 128], BF16) for i in range(2)]
            sg = [sfb(f"sg{i}", [128, 512]) for i in range(2)]
            x1r = [sfb(f"x1r{i}", [128, D]) for i in range(2)]
            yt = [sfb(f"yt{i}", [128, D]) for i in range(2)]
            wd_v = wd_in.rearrange("(fc p) n -> p fc n", p=128)
            cf = {"w": 0, "g": 0, "ps": 0, "s": 0}
            for f0 in range(0, 22, 4):
                nf = min(4, 22 - f0)
                for nb in range(2):
                    i = cf["w"] % 2
                    cf["w"] += 1
                    sc.dma(lambda e, i=i, f0=f0, nf=nf, nb=nb: e.dma_start(out=wst[i][:, 0:nf, :], in_=wd_v[:, f0:f0 + nf, nb * 512:(nb + 1) * 512]),
                           writes=[("wst", i)])
                    sc.pool(lambda e, i=i, f0=f0, nf=nf, nb=nb: e.tensor_copy(out=Wd[:, f0:f0 + nf, nb * 512:(nb + 1) * 512], in_=wst[i][:, 0:nf, :]),
                            reads=[("wst", i)], writes=[("Wd", f0, nb)])
            WD_ALL = [("Wd", f0, nb) for f0 in range(0, 22, 4) for nb in range(2)]
            wg_v = wg_in.rearrange("(kc p) n -> p kc n", p=128)
            wu_v = wu_in.rearrange("(kc p) n -> p kc n", p=128)
            for half in range(2):
                for fcn in range(22):
                    gi = cf["g"] % 2
                    cf["g"] += 1
                    sc.dma(lambda e, gi=gi, fcn=fcn: e.dma_start(out=wgf[gi][:, 0, :, :], in_=wg_v[:, :, fcn * 128:(fcn + 1) * 128]),
                           writes=[("wgf", gi, 0)])
                    sc.dma(lambda e, gi=gi, fcn=fcn: e.dma_start(out=wgf[gi][:, 1, :, :], in_=wu_v[:, :, fcn * 128:(fcn + 1) * 128]),
                           writes=[("wgf", gi, 1)])
                    for w in range(2):
                        sc.pool(lambda e, gi=gi, w=w: e.tensor_tensor(out=wgb[gi][:, w, :, :], in0=wgf[gi][:, w, :, :],
                                                                      in1=g2[:, :].unsqueeze(2).to_broadcast([128, 8, 128]), op=ALU.mult),
                                reads=[("wgf", gi, w), "g2"], writes=[("wgb", gi, w)])
                    for tb in range(2):
                        tok0 = half * 1024 + tb * 512
                        pg = cf["ps"] % 6
                        pu = (cf["ps"] + 1) % 6
                        cf["ps"] += 2
                        for (w, pp) in ((0, pg), (1, pu)):
                            for kc in range(8):
                                sc.pe(lambda e, w=w, pp=pp, kc=kc, gi=gi, tok0=tok0: e.matmul(
                                    psA[:, pp, :], lhsT=wgb[gi][:, w, kc, :], rhs=h2T[:, kc, tok0:tok0 + 512], start=(kc == 0), stop=(kc == 7)),
                                    reads=[("wgb", gi, w), ("h2T", tok0 // 512)], writes=[("psA", pp)])
                        si = cf["s"] % 2
                        cf["s"] += 1
                        sc.act(lambda e, si=si, pg=pg: e.activation(out=sg[si][:, :], in_=psA[:, pg, :], func=AF.Silu),
                               reads=[("psA", pg)], writes=[("sg", si)])
                        sc.dve(lambda e, si=si, pu=pu, fcn=fcn, tb=tb: e.tensor_tensor(out=actT[:, fcn, tb * 512:(tb + 1) * 512], in0=sg[si][:, :],
                                                                                       in1=psA[:, pu, :], op=ALU.mult),
                               reads=[("sg", si), ("psA", pu)], writes=[("actT", fcn)])
                for tl in range(8):
                    t = half * 8 + tl
                    i = t % 2
                    sc.dma(lambda e, i=i, t=t: e.dma_start(out=x1r[i][:, :], in_=x1_s[t * 128:(t + 1) * 128, :]), reads=[("x1_s", t)],
                           writes=[("x1r", i)])
                    pbase = (cf["ps"] % 3) * 2
                    cf["ps"] += 2
                    for nb in range(2):
                        for fcn in range(22):
                            sc.pe(lambda e, nb=nb, fcn=fcn, tl=tl, pbase=pbase: e.matmul(
                                psA[:, pbase + nb, :], lhsT=actT[:, fcn, tl * 128:(tl + 1) * 128], rhs=Wd[:, fcn, nb * 512:(nb + 1) * 512],
                                start=(fcn == 0), stop=(fcn == 21)),
                                reads=[("actT", fcn)] + WD_ALL, writes=[("psA", pbase + nb)])
                    sc.dve(lambda e, i=i, pbase=pbase: e.tensor_tensor(out=yt[i][:, :].rearrange("p (a b) -> p a b", a=2),
                                                                       in0=psA[:, pbase:pbase + 2, :],
                                                                       in1=x1r[i][:, :].rearrange("p (a b) -> p a b", a=2), op=ALU.add),
                           reads=[("psA", pbase), ("psA", pbase + 1), ("x1r", i)], writes=[("yt", i)])
                    tk = sc.dma(lambda e, i=i, t=t: e.dma_start(out=y[t * 128:(t + 1) * 128, :], in_=yt[i][:, :]), reads=[("yt", i)], q="gpsimd")
                    last_out_tokens.append(tk)
        sc.emit(last_out_tokens)

    st.close()
    return nc


def _split_bf16(x, n):
    x = np.asarray(x, np.float64)
    parts = []
    r = x.copy()
    for _ in range(n):
        p = r.astype(np.float32).astype(NPBF)
        parts.append(p)
        r = r - p.astype(np.float64)
    return parts


def host_consts(half):
    c = {}
    c["ident"] = np.eye(128, dtype=np.float32).astype(NPBF)
    t = np.arange(T)
    a, b = t // 64, t % 64
    c["kx"] = np.stack([a, a, a, b - 32, b - 32, np.ones(T), np.ones(T)]).astype(np.float32).astype(NPBF)
    te = 16 * np.arange(256) + 31
    a, b = te // 64, te % 64
    c["kcx"] = np.stack([a, a, a, b - 32, b - 32, np.ones(256), np.ones(256)]).astype(np.float32).astype(NPBF)
    slopes = np.power(2.0, -8.0 * np.arange(1, 17) / 16)
    qx = np.zeros((16, 7, TO), np.float32).astype(NPBF)
    tc = (16 + np.arange(TO) // 128) * 128 + 64
    for h in range(16):
        m = slopes[h]
        A = _split_bf16(np.full(TO, 512.0 * m), 3)
        Bp = _split_bf16(np.full(TO, 8.0 * m), 2)
        C = _split_bf16(-8.0 * m * (tc - 32), 2)
        for r, p in enumerate(A + Bp + C):
            qx[h, r] = p
    c["qx"] = qx
    kk = np.arange(32 * 128).reshape(32, 128)
    es = np.zeros((128, 32, 128), np.float32)
    es[:64] = (kk[None, :, :] // 64 == np.arange(64)[:, None, None])
    c["esel"] = es.astype(NPBF)
    n = np.arange(128)[:, None, None, None] + 128 * np.arange(2)[None, None, :, None]
    tq = (16 + np.arange(16))[None, :, None, None] * 128 + np.arange(128)[None, None, None, :]
    c["cmask"] = (16 * n + 31 <= tq).astype(np.float32).astype(NPBF)
    ik, iq = np.arange(128)[:, None], np.arange(128)[None, :]
    c["mfar"] = (iq < ik).astype(np.float32).astype(NPBF)
    c["mdiag"] = (iq >= ik).astype(np.float32).astype(NPBF)
    cs = np.arange(255) * 16
    ce_ = cs + 31
    bs = np.arange(64) * 64
    be = bs + 63
    bm = ((cs[:, None] <= be[None, :]) & (ce_[:, None] >= bs[None, :])).astype(np.float32)
    bmap = np.zeros((256, 65), np.float32)
    bmap[:255, :64] = bm
    bmap[:, 64] = 1.0
    c["bmap"] = np.ascontiguousarray(bmap.reshape(2, 128, 65).transpose(1, 0, 2)).astype(NPBF)
    tqv = (16 + np.arange(16))[None, :, None] * 128 + np.arange(128)[:, None, None]
    j = np.arange(64)[None, None, :]
    first = 0 if half == 1 else 32
    cur = tqv // 64
    valid = (j * 64 <= tqv) & (j >= first)
    forced = (j == first) | (j == cur) | (j == cur - 1)
    c["bonus"] = np.where(valid, 1e4 * forced, -1e30).astype(np.float32)
    c["hb"] = np.full((128, 1), 0.0 if half == 1 else -30000.0, np.float32)
    c["hf"] = np.full((128, 1), 1.0 if half == 1 else 0.0, np.float32)
    return c


def make_in_maps(inputs):
    x = np.asarray(inputs["x"], dtype=np.float32)
    consts = [host_consts(0), host_consts(1)]
    g1 = np.ascontiguousarray(np.asarray(inputs["norm1_g"], np.float32)[0].reshape(8, 128).T)
    qg = np.ascontiguousarray(np.concatenate([np.asarray(inputs["q_norm_g"], np.float32)[0][None, :],
                                              np.asarray(inputs["k_norm_g"], np.float32)[0]], axis=0).T)
    w_in = np.ascontiguousarray(np.asarray(inputs["w_in"], np.float32)[0])
    f32 = lambda k: np.asarray(inputs[k], np.float32)[0]
    shared = {}
    shared["posT"] = np.ascontiguousarray(np.stack([f32("cmp_pos_k").T, f32("cmp_pos_v").T]))
    shared["cw1"] = np.ascontiguousarray(np.stack([f32("cmp_w1_k"), f32("cmp_w1_v")]))
    shared["cb1"] = np.ascontiguousarray(np.stack([f32("cmp_b1_k").reshape(2, 128).T, f32("cmp_b1_v").reshape(2, 128).T]))
    shared["cw2"] = np.ascontiguousarray(np.stack([f32("cmp_w2_k").reshape(2, 128, 64).transpose(1, 0, 2),
                                                   f32("cmp_w2_v").reshape(2, 128, 64).transpose(1, 0, 2)]))
    shared["cb2c"] = np.ascontiguousarray(np.stack([f32("cmp_b2_k").reshape(64, 1), f32("cmp_b2_v").reshape(64, 1)]))
    shared["cb2r"] = np.ascontiguousarray(np.stack([f32("cmp_b2_k").reshape(1, 64), f32("cmp_b2_v").reshape(1, 64)]))
    cw = f32("conv_w")[:, 0, :]
    shared["lcw"] = np.ascontiguousarray(cw.reshape(4, 8, 128).transpose(2, 1, 0))
    shared["lpar"] = np.ascontiguousarray(np.stack([f32("conv_b"), f32("lru_ba"), f32("lru_bi"), f32("lru_lambda")])
                                          .reshape(4, 8, 128).transpose(2, 0, 1))
    lbd = np.zeros((2, 8, 128, 128), np.float32)
    for w, key in enumerate(("lru_wa", "lru_wi")):
        ww = f32(key)
        for c in range(8):
            lbd[w, c, 0:64, 0:64] = ww[2 * c]
            lbd[w, c, 64:128, 64:128] = ww[2 * c + 1]
    shared["lbd"] = lbd
    for k in ("w_o_attn", "w_o_lru", "w_out", "w_gate", "w_up", "w_down"):
        shared[k] = np.ascontiguousarray(f32(k))
    shared["g2"] = np.ascontiguousarray(f32("norm2_g").reshape(8, 128).T)
    maps = []
    for c in range(8):
        b, half = c // 2, c % 2
        xvv = np.zeros((T, D), np.float32)
        if half == 1:
            xvv[:] = x[b]
        else:
            xvv[TO:] = x[b, :TO]
        m = {"xv": xvv, "w_in": w_in, "g1": g1, "qg": qg}
        m.update(consts[half])
        m.update(shared)
        maps.append(m)
    return maps


def kernel(**inputs):
    nc = build_nc()
    maps = make_in_maps(inputs)
    res = run_bass_kernel_spmd(nc, maps, core_ids=list(range(8)))
    out = np.zeros((4, 4096, D), np.float32)
    for c in range(8):
        b, half = c // 2, c % 2
        out[b, half * TO:(half + 1) * TO] = np.asarray(res.results[c]["y"], np.float32)
    return out
```

```python
from contextlib import ExitStack
import numpy as np
import ml_dtypes
import concourse.bass as bass
import concourse.mybir as mybir
from concourse.bass_utils import run_bass_kernel_spmd

F32 = mybir.dt.float32
BF16 = mybir.dt.bfloat16
AF = mybir.ActivationFunctionType
ALU = mybir.AluOpType
AX = mybir.AxisListType
NPBF = ml_dtypes.bfloat16

D = 1024
T = 4096
TO = 2048
NT = T // 128
IN_DIM = 6704
O_KV = 1024
O_G = 2560
O_LX = 2608
O_LG = 3632
O_MG = 4656
DFF = 2816
EPS = 1e-6
QT_LIMIT = 0
DEFER = [9, 14, 5]
SKIP = ()

COMPUTE = ("tensor", "vector", "scalar", "gpsimd")


class Sched:
    def __init__(self, nc, stack, n_dma_sems=20):
        self.nc = nc
        self.csem = {e: stack.enter_context(nc.semaphore("c_" + e)) for e in COMPUTE}
        self.ccount = {e: 0 for e in COMPUTE}
        self.dsem, self.dround, self.dnext = {}, {}, {}
        for q in ("sync", "gpsimd"):
            self.dsem[q] = [stack.enter_context(nc.semaphore(f"d_{q}{i}")) for i in range(n_dma_sems)]
            self.dround[q] = [0] * n_dma_sems
            self.dnext[q] = 0
        self.waited = {}
        self.reset_block()

    def reset_block(self):
        self.ops = []
        self.last_w = {}
        self.readers = {}

    def _add(self, eng, kind, fn, reads, writes, nosync_same=False):
        deps = []
        for k in reads:
            if k in self.last_w:
                deps.append(self.last_w[k])
        for k in writes:
            if k in self.last_w:
                deps.append(self.last_w[k])
            deps.extend(self.readers.get(k, ()))
        pre = None
        if kind == "dma":
            i = self.dnext[eng]
            self.dnext[eng] = (i + 1) % len(self.dsem[eng])
            sem = self.dsem[eng][i]
            if self.dround[eng][i] > 0:
                pre = (sem, 16 * self.dround[eng][i], None)
            self.dround[eng][i] += 1
            tok = (sem, 16 * self.dround[eng][i], None)
            inc = 16
        else:
            self.ccount[eng] += 1
            tok = (self.csem[eng], self.ccount[eng], eng)
            inc = 1
        dd = [d for d in deps if not (nosync_same and d[2] == eng and kind != "dma")]
        if pre is not None:
            dd.append(pre)
        self.ops.append((eng, fn, dd, tok[0], inc))
        for k in writes:
            self.last_w[k] = tok
            self.readers[k] = []
        for k in reads:
            self.readers.setdefault(k, []).append(tok)
        return tok

    def pe(self, fn, reads=(), writes=()):
        return self._add("tensor", "c", fn, reads, writes, nosync_same=True)

    def act(self, fn, reads=(), writes=()):
        return self._add("scalar", "c", fn, reads, writes)

    def dve(self, fn, reads=(), writes=()):
        return self._add("vector", "c", fn, reads, writes)

    def pool(self, fn, reads=(), writes=()):
        return self._add("gpsimd", "c", fn, reads, writes)

    def dma(self, fn, reads=(), writes=(), q="sync"):
        return self._add(q, "dma", fn, reads, writes)

    def emit(self, final_tokens=()):
        nc = self.nc
        per = {}
        for op in self.ops:
            per.setdefault(op[0], []).append(op)
        waited = self.waited

        def run(engname, eng, extra):
            for (_, fn, deps, sem, inc) in per.get(engname, []):
                need = {}
                for (s, v, _) in deps:
                    if need.get(id(s), (None, 0))[1] < v:
                        need[id(s)] = (s, v)
                for key, (s, v) in need.items():
                    if waited.get((engname, key), 0) >= v:
                        continue
                    eng.wait_ge(s, v)
                    waited[(engname, key)] = v
                fn(eng).then_inc(sem, inc)
            for (s, v, _) in extra:
                eng.wait_ge(s, v)

        drain = []
        for q in ("sync", "gpsimd"):
            for i, sem in enumerate(self.dsem[q]):
                if self.dround[q][i] > 0:
                    drain.append((sem, 16 * self.dround[q][i], None))
        extra = list(final_tokens) + drain
        with nc.Block() as block:
            names = list(per.keys())
            if "sync" not in names:
                names.append("sync")
            for engname in names:
                def f(eng, engname=engname):
                    run(engname, eng, extra if engname == "sync" else ())
                getattr(block, engname)(f)
        self.reset_block()


def build_nc(upto="all", debug=()):
    nc = bass.Bass("TRN2", target_bir_lowering=False)
    st = ExitStack()

    def din(name, shape, dt=F32):
        return nc.dram_tensor(name, list(shape), dt, kind="ExternalInput").ap()

    def dscratch(name, shape, dt=F32):
        kind = "ExternalOutput" if name in debug else "Internal"
        return nc.dram_tensor(name, list(shape), dt, kind=kind).ap()

    xv = din("xv", [T, D])
    w_in = din("w_in", [D, IN_DIM])
    g1_in = din("g1", [128, 8])
    ident_in = din("ident", [128, 128], BF16)
    qg_in = din("qg", [64, 4])
    y = nc.dram_tensor("y", [TO, D], F32, kind="ExternalOutput").ap()
    posT_in = din("posT", [2, 64, 32])
    w1_in = din("cw1", [2, 32, 64, 256])
    b1_in = din("cb1", [2, 128, 2])
    w2_in = din("cw2", [2, 128, 2, 64])
    b2c_in = din("cb2c", [2, 64, 1])
    b2r_in = din("cb2r", [2, 1, 64])
    kx_in = din("kx", [7, T], BF16)
    kcx_in = din("kcx", [7, 256], BF16)
    qx_in = din("qx", [16, 7, TO], BF16)
    esel_in = din("esel", [128, 32, 128], BF16)
    cmask_in = din("cmask", [128, 16, 2, 128], BF16)
    mfar_in = din("mfar", [128, 128], BF16)
    mdiag_in = din("mdiag", [128, 128], BF16)
    bmap_in = din("bmap", [128, 2, 65], BF16)
    bonus_in = din("bonus", [128, 16, 64])
    hb_in = din("hb", [128, 1])
    hf_in = din("hf", [128, 1])
    woa_in = din("w_o_attn", [D, D])
    wol_in = din("w_o_lru", [D, D])
    wout_in = din("w_out", [D, D])
    g2_in = din("g2", [128, 8])
    wg_in = din("w_gate", [D, DFF])
    wu_in = din("w_up", [D, DFF])
    wd_in = din("w_down", [DFF, D])
    lcw_in = din("lcw", [128, 8, 4])
    lpar_in = din("lpar", [128, 4, 8])
    lbd_in = din("lbd", [2, 8, 128, 128])

    qT_s = dscratch("qT_s", [16, 64, TO], BF16)
    kT_s = dscratch("kT_s", [2, 4, 64, T], BF16)
    ckT_s = dscratch("ckT_s", [2, 4, 64, T], BF16)
    v_s = dscratch("v_s", [2, T, 256], BF16)
    gate_s = dscratch("gate_s", [48, TO], F32)
    lx_s = dscratch("lx_s", [8, 128, T], F32)
    lg_s = dscratch("lg_s", [8, 128, TO], F32)
    mg_s = dscratch("mg_s", [TO, 2048], F32)
    kcT_s = dscratch("kcT_s", [4, 64, 256], BF16)
    vc_s = dscratch("vc_s", [256, 4, 64], BF16)
    attnT_s = dscratch("attnT_s", [16, 64, TO], BF16)
    lruT_s = dscratch("lruT_s", [8, 128, TO], BF16)
    x1_s = dscratch("x1_s", [TO, D], F32)

    sc = Sched(nc, st)
    sb = lambda name, shape, dt=F32: st.enter_context(nc.sbuf_tensor("sb_" + name, list(shape), dt))
    psA = st.enter_context(nc.psum_tensor("psA", [128, 6, 512], F32))
    psT = st.enter_context(nc.psum_tensor("psT", [128, 1, 1024], BF16))
    psT_f = st.enter_context(nc.psum_tensor("psT_f", [128, 4, 65], F32))

    ident = sb("ident", [128, 128], BF16)
    g1 = sb("g1sb", [128, 8])
    qg = sb("qgsb", [64, 4])
    ones64 = sb("ones64", [64, 64])
    epsb = sb("epsb", [128, 1])
    hst = ExitStack()
    hT = hst.enter_context(nc.sbuf_tensor("sb_hT", [128, 8, T], BF16))

    last_out_tokens = []

    sc.dma(lambda e: e.dma_start(out=ident[:, :], in_=ident_in), writes=["ident"])
    sc.dma(lambda e: e.dma_start(out=g1[:, :], in_=g1_in), writes=["g1"])
    sc.dma(lambda e: e.dma_start(out=qg[:, :], in_=qg_in), writes=["qg"])
    sc.dve(lambda e: e.memset(ones64[:, :], 1.0), writes=["ones64"])
    sc.dve(lambda e: e.memset(epsb[:, :], EPS), writes=["epsb"])

    with ExitStack() as pa:
        sba = lambda name, shape, dt=F32: pa.enter_context(nc.sbuf_tensor("sa_" + name, list(shape), dt))
        xt = [sba(f"xt{i}", [128, D]) for i in range(3)]
        sq = [sba(f"sq{i}", [128, D]) for i in range(2)]
        xn = [sba(f"xn{i}", [128, D], BF16) for i in range(2)]
        ss = [sba(f"ss{i}", [128, 1]) for i in range(2)]
        for t in range(NT):
            a, b2 = t % 3, t % 2
            sc.dma(lambda e, a=a, t=t: e.dma_start(out=xt[a][:, :], in_=xv[t * 128:(t + 1) * 128, :]),
                   writes=[("xt", a)])
            sc.dve(lambda e, a=a, b2=b2: e.tensor_tensor(out=sq[b2][:, :], in0=xt[a][:, :], in1=xt[a][:, :], op=ALU.mult),
                   reads=[("xt", a)], writes=[("sq", b2)])
            sc.dve(lambda e, b2=b2: e.tensor_reduce(out=ss[b2][:, :], in_=sq[b2][:, :], axis=AX.X, op=ALU.add),
                   reads=[("sq", b2)], writes=[("ss", b2)])
            sc.dve(lambda e, b2=b2: e.tensor_scalar(out=ss[b2][:, :], in0=ss[b2][:, :], scalar1=1.0 / D, scalar2=EPS,
                                                    op0=ALU.mult, op1=ALU.add),
                   reads=[("ss", b2)], writes=[("ss", b2)])
            sc.dve(lambda e, b2=b2: e.reciprocal(out=ss[b2][:, :], in_=ss[b2][:, :]),
                   reads=[("ss", b2)], writes=[("ss", b2)])
            sc.act(lambda e, b2=b2: e.activation(out=ss[b2][:, :], in_=ss[b2][:, :], func=AF.Sqrt),
                   reads=[("ss", b2)], writes=[("ss", b2)])
            sc.act(lambda e, a=a, b2=b2: e.mul(out=xn[b2][:, :], in_=xt[a][:, :], mul=ss[b2][:, 0:1]),
                   reads=[("xt", a), ("ss", b2)], writes=[("xn", b2)])
            for kc in range(8):
                sc.pe(lambda e, b2=b2, kc=kc: e.transpose(out=psT[:, 0, kc * 128:(kc + 1) * 128],
                                                          in_=xn[b2][:, kc * 128:(kc + 1) * 128], identity=ident[:, :]),
                      reads=[("xn", b2), "ident"], writes=[("psT", 0)])
            cp = sc.dve if t % 2 == 0 else sc.act
            if t % 2 == 0:
                sc.dve(lambda e, b2=b2, t=t: e.tensor_copy(out=hT[:, :, t * 128:(t + 1) * 128],
                                                           in_=psT[:, 0, :].rearrange("p (k n) -> p k n", k=8)),
                       reads=[("psT", 0)], writes=[("hT", t // 4)])
            else:
                sc.act(lambda e, b2=b2, t=t: e.copy(out=hT[:, :, t * 128:(t + 1) * 128],
                                                    in_=psT[:, 0, :].rearrange("p (k n) -> p k n", k=8)),
                       reads=[("psT", 0)], writes=[("hT", t // 4)])
        if upto == "A":
            dbg = dscratch("hT_dbg", [128, 8, T], BF16)
            tk = sc.dma(lambda e: e.dma_start(out=dbg, in_=hT[:, :, :]), reads=[("hT", i) for i in range(8)])
            sc.emit([tk])
            pa.close()
            hst.close()
            st.close()
            return nc
        sc.emit()

    with ExitStack() as pb:
        sbb = lambda name, shape, dt=F32: pb.enter_context(nc.sbuf_tensor("sbb_" + name, list(shape), dt))
        wf = [sbb(f"wf{i}", [128, 8, 128]) for i in range(2)]
        wb = [sbb(f"wb{i}", [128, 8, 128], BF16) for i in range(2)]
        wf2 = [sbb(f"wf2{i}", [128, 8, 512]) for i in range(2)]
        wb2 = [sbb(f"wb2{i}", [128, 8, 512], BF16) for i in range(2)]
        ev = [sbb(f"ev{i}", [128, 512]) for i in range(3)]
        evb = [sbb(f"evb{i}", [128, 512], BF16) for i in range(3)]
        sqbb = [sbb(f"sqbb{i}", [128, 512], BF16) for i in range(2)]
        ones128b = sbb("ones128b", [128, 64], BF16)
        sc.dve(lambda e: e.memset(ones128b[:, :], 1.0), writes=["ones128b"])
        for i in range(2):
            sc.pool(lambda e, i=i: e.memset(sqbb[i][64:128, :], 0.0), writes=[("sqz", i)])
        rsb = [sbb(f"rsb{i}", [64, 512]) for i in range(2)]
        w_in_v = w_in.rearrange("(kc p) n -> p kc n", p=128)
        cnt = {"w": 0, "ps": 0, "ev": 0, "n": 0, "w2": 0}

        def load_w(col0, M):
            s = cnt["w"] % 2
            cnt["w"] += 1
            sc.dma(lambda e: e.dma_start(out=wf[s][:, :, 0:M], in_=w_in_v[:, :, col0:col0 + M]), writes=[("wf", s)])
            sc.pool(lambda e: e.tensor_tensor(out=wb[s][:, :, 0:M], in0=wf[s][:, :, 0:M],
                                              in1=g1[:, :].unsqueeze(2).to_broadcast([128, 8, M]), op=ALU.mult),
                    reads=[("wf", s), "g1"], writes=[("wb", s)])
            return s

        def mm_fm(s, M, tb):
            p = cnt["ps"] % 4
            cnt["ps"] += 1
            for kc in range(8):
                sc.pe(lambda e, kc=kc: e.matmul(psA[0:M, p, :], lhsT=wb[s][:, kc, 0:M], rhs=hT[:, kc, tb * 512:(tb + 1) * 512],
                                                start=(kc == 0), stop=(kc == 7)),
                      reads=[("wb", s), ("hT", tb)], writes=[("psA", p)])
            flush_pending()
            return p

        pending = []

        def flush_pending():
            while pending:
                pending.pop(0)()

        def evac_norm(p, gcol, dst):
            n = cnt["n"] % 2
            cnt["n"] += 1
            e3 = cnt["ev"] % 3
            cnt["ev"] += 1
            sc.act(lambda e: e.activation(out=sqbb[n][0:64, :], in_=psA[0:64, p, :], func=AF.Square),
                   reads=[("psA", p), ("sqz", n)], writes=[("sqb", n)])

            def part_b():
                sc.pe(lambda e: e.matmul(psA[0:64, 4 + n, :], lhsT=ones128b[:, :], rhs=sqbb[n][:, :], start=True, stop=True),
                      reads=[("sqb", n), ("sqz", n), "ones128b"], writes=[("psA", 4 + n)])
                sc.act(lambda e: e.activation(out=rsb[n][:, :], in_=psA[0:64, 4 + n, :], func=AF.Ln, scale=1.0 / 64, bias=epsb[0:64, 0:1]),
                       reads=[("psA", 4 + n), "epsb"], writes=[("rsb", n)])
                sc.act(lambda e: e.activation(out=rsb[n][:, :], in_=rsb[n][:, :], func=AF.Exp, scale=-0.5),
                       reads=[("rsb", n)], writes=[("rsb", n)])
                sc.dve(lambda e: e.scalar_tensor_tensor(out=evb[e3][0:64, :], in0=psA[0:64, p, :], scalar=qg[:, gcol:gcol + 1],
                                                        in1=rsb[n][:, :], op0=ALU.mult, op1=ALU.mult),
                       reads=[("psA", p), ("rsb", n), "qg"], writes=[("evb", e3)])
                sc.dma(lambda e: e.dma_start(out=dst, in_=evb[e3][0:64, :]), reads=[("evb", e3)], q="gpsimd")
            pending.append(part_b)

        def evac_copy_bf(p, M, dst):
            e3 = cnt["ev"] % 3
            cnt["ev"] += 1
            sc.act(lambda e: e.copy(out=evb[e3][0:M, :], in_=psA[0:M, p, :]), reads=[("psA", p)], writes=[("evb", e3)])
            sc.dma(lambda e: e.dma_start(out=dst, in_=evb[e3][0:M, :]), reads=[("evb", e3)], q="gpsimd")

        def evac_f32(p, M, dst, func=None):
            e3 = cnt["ev"] % 3
            cnt["ev"] += 1
            if func is None:
                sc.dve(lambda e: e.tensor_copy(out=ev[e3][0:M, :], in_=psA[0:M, p, :]), reads=[("psA", p)], writes=[("ev", e3)])
            else:
                sc.act(lambda e: e.activation(out=ev[e3][0:M, :], in_=psA[0:M, p, :], func=func),
                       reads=[("psA", p)], writes=[("ev", e3)])
            sc.dma(lambda e: e.dma_start(out=dst, in_=ev[e3][0:M, :]), reads=[("ev", e3)], q="gpsimd")

        for h in range(16):
            s = load_w(h * 64, 64)
            for tb in range(4, 8):
                p = mm_fm(s, 64, tb)
                evac_norm(p, 0, qT_s[h, :, (tb - 4) * 512:(tb - 3) * 512])
        for si, (kset, gcol) in enumerate(((2, 2), (4, 3))):
            for g in range(4):
                s = load_w(O_KV + kset * 256 + g * 64, 64)
                for tb in range(8):
                    p = mm_fm(s, 64, tb)
                    evac_norm(p, gcol, kT_s[si, g, :, tb * 512:(tb + 1) * 512])
        for si in range(2):
            for g in range(4):
                s = load_w(O_KV + si * 256 + g * 64, 64)
                for tb in range(8):
                    p = mm_fm(s, 64, tb)
                    evac_copy_bf(p, 64, ckT_s[si, g, :, tb * 512:(tb + 1) * 512])
        s = load_w(O_G, 48)
        for tb in range(4, 8):
            p = mm_fm(s, 48, tb)
            evac_f32(p, 48, gate_s[:, (tb - 4) * 512:(tb - 3) * 512], func=AF.Sigmoid)
        for c in range(8):
            s = load_w(O_LX + c * 128, 128)
            for tb in range(8):
                p = mm_fm(s, 128, tb)
                evac_f32(p, 128, lx_s[c, :, tb * 512:(tb + 1) * 512])
        for c in range(8):
            s = load_w(O_LG + c * 128, 128)
            for tb in range(4, 8):
                p = mm_fm(s, 128, tb)
                evac_f32(p, 128, lg_s[c, :, (tb - 4) * 512:(tb - 3) * 512])

        flush_pending()

        def load_w2(cols):
            s = cnt["w2"] % 2
            cnt["w2"] += 1
            off = 0
            for (c0, m) in cols:
                sc.dma(lambda e, c0=c0, m=m, off=off: e.dma_start(out=wf2[s][:, :, off:off + m], in_=w_in_v[:, :, c0:c0 + m]),
                       writes=[("wf2", s, off)])
                off += m
            sc.pool(lambda e: e.tensor_tensor(out=wb2[s][:, :, :], in0=wf2[s][:, :, :],
                                              in1=g1[:, :].unsqueeze(2).to_broadcast([128, 8, 512]), op=ALU.mult),
                    reads=[("wf2", s, o) for o in (0, 256)] + ["g1"], writes=[("wb2", s)])
            return s

        def mm_tm(s, t):
            p = cnt["ps"] % 4
            cnt["ps"] += 1
            for kc in range(8):
                sc.pe(lambda e, kc=kc: e.matmul(psA[:, p, :], lhsT=hT[:, kc, t * 128:(t + 1) * 128], rhs=wb2[s][:, kc, :],
                                                start=(kc == 0), stop=(kc == 7)),
                      reads=[("wb2", s), ("hT", t // 4)], writes=[("psA", p)])
            return p

        s = load_w2([(O_KV + 3 * 256, 256), (O_KV + 5 * 256, 256)])
        for t in range(NT):
            p = mm_tm(s, t)
            e3 = cnt["ev"] % 3
            cnt["ev"] += 1
            sc.act(lambda e, p=p, e3=e3: e.copy(out=evb[e3][:, :], in_=psA[:, p, :]), reads=[("psA", p)], writes=[("evb", e3)])
            sc.dma(lambda e, e3=e3, t=t: e.dma_start(out=v_s[:, t * 128:(t + 1) * 128, :].rearrange("s t c -> t s c"),
                                                     in_=evb[e3][:, :].rearrange("t (s c) -> t s c", s=2)),
                   reads=[("evb", e3)], q="gpsimd")
        for mc in range(4):
            s = load_w2([(O_MG + mc * 512, 256), (O_MG + mc * 512 + 256, 256)])
            for t in range(16, NT):
                p = mm_tm(s, t)
                evac_f32(p, 128, mg_s[(t - 16) * 128:(t - 15) * 128, mc * 512:(mc + 1) * 512], func=AF.Sigmoid)
        if upto == "B":
            sc.emit()
            pb.close()
            hst.close()
            st.close()
            return nc
        sc.emit()
    hst.close()


    with ExitStack() as pc:
        sbc = lambda name, shape, dt=F32: pc.enter_context(nc.sbuf_tensor("sc_" + name, list(shape), dt))
        ck = sbc("ck", [64, 4, T + 32], BF16)
        posf = sbc("posf", [64, 32])
        w1f = sbc("w1f", [64, 32, 256])
        w1b = sbc("w1b", [64, 32, 256], BF16)
        w2f = sbc("w2f", [128, 2, 64])
        w2b = sbc("w2b", [128, 2, 64], BF16)
        b1c = sbc("b1c", [128, 2])
        b2c = sbc("b2c", [64, 1])
        b2rf = sbc("b2rf", [1, 64])
        b2rb = sbc("b2rb", [1, 64], BF16)
        onesr = sbc("onesr", [1, 128], BF16)
        hidb = sbc("hidb", [128, 2, 4, 256], BF16)
        biasc = [sbc(f"biasc{i}", [128, 1]) for i in range(2)]
        gu = [sbc(f"gu{i}", [128, 256]) for i in range(2)]
        gt = [sbc(f"gt{i}", [128, 256]) for i in range(2)]
        kf = [sbc(f"kf{i}", [64, 256]) for i in range(2)]
        ksq = [sbc(f"ksq{i}", [64, 256]) for i in range(2)]
        krs = [sbc(f"krs{i}", [64, 256]) for i in range(2)]
        kob = [sbc(f"kob{i}", [128, 256], BF16) for i in range(2)]
        sc.dve(lambda e: e.memset(onesr[:, :], 1.0), writes=["onesr"])
        cc = {"ps": 0, "g": 0, "k": 0}
        for si in range(2):
            sc.dma(lambda e, si=si: e.dma_start(out=ck[:, :, 0:T], in_=ckT_s[si].rearrange("g d t -> d g t")), writes=["ck"])
            sc.dma(lambda e, si=si: e.dma_start(out=posf[:, :], in_=posT_in[si]), writes=["posf"])
            sc.act(lambda e: e.copy(out=ck[:, :, T:T + 32], in_=posf[:, :].unsqueeze(1).to_broadcast([64, 4, 32])),
                   reads=["posf"], writes=["ck"])
            sc.dma(lambda e, si=si: e.dma_start(out=w1f[:, :, :], in_=w1_in[si].rearrange("l d f -> d l f")), writes=["w1f"])
            sc.pool(lambda e: e.tensor_copy(out=w1b[:, :, :], in_=w1f[:, :, :]), reads=["w1f"], writes=["w1b"])
            sc.dma(lambda e, si=si: e.dma_start(out=w2f[:, :, :], in_=w2_in[si]), writes=["w2f"])
            sc.pool(lambda e: e.tensor_copy(out=w2b[:, :, :], in_=w2f[:, :, :]), reads=["w2f"], writes=["w2b"])
            sc.dma(lambda e, si=si: e.dma_start(out=b1c[:, :], in_=b1_in[si]), writes=["b1c"])
            sc.dma(lambda e, si=si: e.dma_start(out=b2c[:, :], in_=b2c_in[si]), writes=["b2c"])
            sc.dma(lambda e, si=si: e.dma_start(out=b2rf[:, :], in_=b2r_in[si]), writes=["b2rf"])
            sc.dve(lambda e: e.tensor_copy(out=b2rb[:, :], in_=b2rf[:, :]), reads=["b2rf"], writes=["b2rb"])
            for g in range(4):
                for fc in range(2):
                    p = cc["ps"] % 4
                    cc["ps"] += 1
                    for l in range(32):
                        sc.pe(lambda e, p=p, l=l, g=g, fc=fc: e.matmul(
                            psA[:, p, 0:257], lhsT=w1b[:, l, fc * 128:(fc + 1) * 128], rhs=ck[:, g, l:l + 4097:16],
                            start=(l == 0), stop=(l == 31)), reads=["w1b", "ck"], writes=[("psA", p)])
                    i2 = cc["g"] % 2
                    cc["g"] += 1
                    sc.dve(lambda e, p=p, fc=fc, i2=i2: e.tensor_tensor(out=biasc[i2][:, :], in0=psA[:, p, 256:257],
                                                                        in1=b1c[:, fc:fc + 1], op=ALU.add),
                           reads=[("psA", p), "b1c"], writes=[("biasc", i2)])
                    sc.act(lambda e, p=p, i2=i2: e.activation(out=gu[i2][:, :], in_=psA[:, p, 0:256], func=AF.Identity,
                                                              bias=biasc[i2][:, 0:1]),
                           reads=[("psA", p), ("biasc", i2)], writes=[("gu", i2)])
                    sc.dve(lambda e, i2=i2: e.tensor_tensor(out=gt[i2][:, :], in0=gu[i2][:, :], in1=gu[i2][:, :], op=ALU.mult),
                           reads=[("gu", i2)], writes=[("gt", i2)])
                    sc.dve(lambda e, i2=i2: e.tensor_scalar(out=gt[i2][:, :], in0=gt[i2][:, :], scalar1=0.044715, scalar2=1.0,
                                                            op0=ALU.mult, op1=ALU.add),
                           reads=[("gt", i2)], writes=[("gt", i2)])
                    sc.dve(lambda e, i2=i2: e.tensor_tensor(out=gt[i2][:, :], in0=gt[i2][:, :], in1=gu[i2][:, :], op=ALU.mult),
                           reads=[("gt", i2), ("gu", i2)], writes=[("gt", i2)])
                    sc.act(lambda e, i2=i2: e.activation(out=gt[i2][:, :], in_=gt[i2][:, :], func=AF.Sigmoid, scale=1.5957691216),
                           reads=[("gt", i2)], writes=[("gt", i2)])
                    sc.dve(lambda e, i2=i2, fc=fc, g=g: e.tensor_tensor(out=hidb[:, fc, g, :], in0=gt[i2][:, :], in1=gu[i2][:, :],
                                                                        op=ALU.mult),
                           reads=[("gt", i2), ("gu", i2)], writes=[("hidb", g)])
                if si == 0:
                    p = cc["ps"] % 4
                    cc["ps"] += 1
                    for fc in range(2):
                        sc.pe(lambda e, p=p, fc=fc, g=g: e.matmul(psA[0:64, p, 0:256], lhsT=w2b[:, fc, :], rhs=hidb[:, fc, g, :],
                                                                  start=(fc == 0), stop=(fc == 1)),
                              reads=["w2b", ("hidb", g)], writes=[("psA", p)])
                    k2 = cc["k"] % 2
                    cc["k"] += 1
                    sc.act(lambda e, p=p, k2=k2: e.activation(out=kf[k2][:, :], in_=psA[0:64, p, 0:256], func=AF.Identity,
                                                              bias=b2c[:, 0:1]),
                           reads=[("psA", p), "b2c"], writes=[("kf", k2)])
                    sc.act(lambda e, k2=k2: e.activation(out=ksq[k2][:, :], in_=kf[k2][:, :], func=AF.Square),
                           reads=[("kf", k2)], writes=[("ksq", k2)])
                    sc.pe(lambda e, k2=k2: e.matmul(psA[0:64, 4 + k2, 0:256], lhsT=ones64[:, :], rhs=ksq[k2][:, :], start=True, stop=True),
                          reads=[("ksq", k2), "ones64"], writes=[("psA", 4 + k2)])
                    sc.dve(lambda e, k2=k2: e.tensor_scalar(out=krs[k2][:, :], in0=psA[0:64, 4 + k2, 0:256], scalar1=1.0 / 64,
                                                            scalar2=EPS, op0=ALU.mult, op1=ALU.add),
                           reads=[("psA", 4 + k2)], writes=[("krs", k2)])
                    sc.dve(lambda e, k2=k2: e.reciprocal(out=krs[k2][:, :], in_=krs[k2][:, :]), reads=[("krs", k2)], writes=[("krs", k2)])
                    sc.act(lambda e, k2=k2: e.activation(out=krs[k2][:, :], in_=krs[k2][:, :], func=AF.Sqrt),
                           reads=[("krs", k2)], writes=[("krs", k2)])
                    sc.dve(lambda e, k2=k2: e.scalar_tensor_tensor(out=kob[k2][0:64, :], in0=kf[k2][:, :], scalar=qg[:, 1:2],
                                                                   in1=krs[k2][:, :], op0=ALU.mult, op1=ALU.mult),
                           reads=[("kf", k2), ("krs", k2), "qg"], writes=[("kob", k2)])
                    sc.dma(lambda e, k2=k2, g=g: e.dma_start(out=kcT_s[g], in_=kob[k2][0:64, :]), reads=[("kob", k2)], q="gpsimd")
                else:
                    for nt in range(2):
                        p = cc["ps"] % 4
                        cc["ps"] += 1
                        for fc in range(2):
                            sc.pe(lambda e, p=p, fc=fc, g=g, nt=nt: e.matmul(
                                psA[:, p, 0:64], lhsT=hidb[:, fc, g, nt * 128:(nt + 1) * 128], rhs=w2b[:, fc, :],
                                start=(fc == 0), stop=False), reads=["w2b", ("hidb", g)], writes=[("psA", p)])
                        sc.pe(lambda e, p=p: e.matmul(psA[:, p, 0:64], lhsT=onesr[:, :], rhs=b2rb[:, :], start=False, stop=True),
                              reads=["onesr", "b2rb"], writes=[("psA", p)])
                        k2 = cc["k"] % 2
                        cc["k"] += 1
                        sc.act(lambda e, p=p, k2=k2: e.copy(out=kob[k2][:, 0:64], in_=psA[:, p, 0:64]),
                               reads=[("psA", p)], writes=[("kob", k2)])
                        sc.dma(lambda e, k2=k2, g=g, nt=nt: e.dma_start(out=vc_s[nt * 128:(nt + 1) * 128, g, :], in_=kob[k2][:, 0:64]),
                               reads=[("kob", k2)], q="gpsimd")
        if upto == "C":
            sc.emit()
            pc.close()
            st.close()
            return nc
        sc.emit()

    if "D" not in SKIP:
      with ExitStack() as pd:
        sbd = lambda name, shape, dt=F32: pd.enter_context(nc.sbuf_tensor("sd_" + name, list(shape), dt))
        lx = [sbd(f"lx{i}", [128, T + 3]) for i in range(2)]
        xcs = [sbd(f"xc{i}", [128, T]) for i in range(2)]
        rbufs = [sbd(f"rbuf{i}", [128, T]) for i in range(2)]
        ibufs = [sbd(f"ibuf{i}", [128, T]) for i in range(2)]
        tbuf = sbd("tbuf", [128, T])
        hbuf = sbd("hbuf", [128, T])
        lg = [sbd(f"lg{i}", [128, TO]) for i in range(2)]
        gl = sbd("gl", [128, TO])
        lob = [sbd(f"lob{i}", [128, TO], BF16) for i in range(2)]
        lcw = sbd("lcw", [128, 8, 4])
        lpar = sbd("lpar", [128, 4, 8])
        hf = sbd("hf", [128, 1])
        bd = [sbd(f"bd{i}", [128, 2, 128]) for i in range(2)]
        sp = sbd("sp", [128, 8])
        spx = sbd("spx", [128, 8])
        spl = sbd("spl", [128, 8])
        spt = sbd("spt", [128, 8])
        spm = sbd("spm", [128, 8])
        sc8 = sbd("sc8", [128, 8])
        sc.dma(lambda e: e.dma_start(out=lcw[:, :, :], in_=lcw_in), writes=["lcw"])
        sc.dma(lambda e: e.dma_start(out=lpar[:, :, :], in_=lpar_in), writes=["lpar"])
        sc.dma(lambda e: e.dma_start(out=hf[:, :], in_=hf_in), writes=["hf"])
        for i in range(2):
            sc.dve(lambda e, i=i: e.memset(lx[i][:, 0:3], 0.0), writes=[("lxz", i)])
        lam = lpar[:, 3, :]
        sc.dve(lambda e: e.tensor_scalar(out=spx[:, :], in0=lam, scalar1=-1.0, scalar2=None, op0=ALU.mult), reads=["lpar"], writes=["spx"])
        sc.dve(lambda e: e.tensor_tensor(out=spx[:, :], in0=spx[:, :], in1=lam, op=ALU.max), reads=["lpar", "spx"], writes=["spx"])
        sc.act(lambda e: e.activation(out=spx[:, :], in_=spx[:, :], func=AF.Exp, scale=-1.0), reads=["spx"], writes=["spx"])
        sc.act(lambda e: e.activation(out=spl[:, :], in_=spx[:, :], func=AF.Ln, bias=1.0), reads=["spx"], writes=["spl"])
        sc.dve(lambda e: e.tensor_scalar(out=spt[:, :], in0=spx[:, :], scalar1=-0.25, scalar2=1.0 / 3.0, op0=ALU.mult, op1=ALU.add),
               reads=["spx"], writes=["spt"])
        sc.dve(lambda e: e.tensor_tensor(out=spt[:, :], in0=spt[:, :], in1=spx[:, :], op=ALU.mult), reads=["spt", "spx"], writes=["spt"])
        sc.dve(lambda e: e.tensor_scalar(out=spt[:, :], in0=spt[:, :], scalar1=-1.0, scalar2=0.5, op0=ALU.mult, op1=ALU.add),
               reads=["spt"], writes=["spt"])
        sc.dve(lambda e: e.tensor_tensor(out=spt[:, :], in0=spt[:, :], in1=spx[:, :], op=ALU.mult), reads=["spt", "spx"], writes=["spt"])
        sc.dve(lambda e: e.tensor_scalar(out=spt[:, :], in0=spt[:, :], scalar1=-1.0, scalar2=1.0, op0=ALU.mult, op1=ALU.add),
               reads=["spt"], writes=["spt"])
        sc.dve(lambda e: e.tensor_tensor(out=spt[:, :], in0=spt[:, :], in1=spx[:, :], op=ALU.mult), reads=["spt", "spx"], writes=["spt"])
        sc.dve(lambda e: e.tensor_single_scalar(out=spm[:, :], in_=spx[:, :], scalar=0.03, op=ALU.is_lt), reads=["spx"], writes=["spm"])
        sc.dve(lambda e: e.tensor_tensor(out=spt[:, :], in0=spt[:, :], in1=spl[:, :], op=ALU.subtract), reads=["spt", "spl"], writes=["spt"])
        sc.dve(lambda e: e.tensor_tensor(out=spt[:, :], in0=spt[:, :], in1=spm[:, :], op=ALU.mult), reads=["spt", "spm"], writes=["spt"])
        sc.dve(lambda e: e.tensor_tensor(out=spl[:, :], in0=spl[:, :], in1=spt[:, :], op=ALU.add), reads=["spt", "spl"], writes=["spl"])
        sc.dve(lambda e: e.tensor_scalar(out=sp[:, :], in0=lam, scalar1=-1.0, scalar2=0.0, op0=ALU.mult, op1=ALU.max),
               reads=["lpar"], writes=["sp"])
        sc.dve(lambda e: e.tensor_tensor(out=sp[:, :], in0=sp[:, :], in1=spl[:, :], op=ALU.add), reads=["sp", "spl"], writes=["sp"])
        sc.dve(lambda e: e.tensor_scalar(out=sc8[:, :], in0=sp[:, :], scalar1=-8.0, scalar2=None, op0=ALU.mult), reads=["sp"], writes=["sc8"])
        cd = {"ps": 0}
        def d_stage1(c):
            li = c % 2
            sc.dma(lambda e, li=li, c=c: e.dma_start(out=lx[li][:, 3:3 + T], in_=lx_s[c]), writes=[("lx", li)])
            sc.dma(lambda e, li=li, c=c: e.dma_start(out=bd[li][:, :, :], in_=lbd_in[:, c, :, :].rearrange("w k j -> k w j")),
                   writes=[("bd", li)])
            sc.dma(lambda e, li=li, c=c: e.dma_start(out=lg[li][:, :], in_=lg_s[c]), writes=[("lg", li)])
            sc.dve(lambda e, li=li, c=c: e.tensor_scalar(out=xcs[li][:, :], in0=lx[li][:, 0:T], scalar1=lcw[:, c, 0:1], scalar2=lpar[:, 0, c:c + 1],
                                                         op0=ALU.mult, op1=ALU.add),
                   reads=[("lx", li), ("lxz", li), "lcw", "lpar"], writes=[("xc", li)])
            for j in range(1, 4):
                sc.dve(lambda e, li=li, c=c, j=j: e.scalar_tensor_tensor(out=xcs[li][:, :], in0=lx[li][:, j:j + T], scalar=lcw[:, c, j:j + 1],
                                                                         in1=xcs[li][:, :], op0=ALU.mult, op1=ALU.add),
                       reads=[("lx", li), ("lxz", li), "lcw", ("xc", li)], writes=[("xc", li)])
            for (w, dstb, nm, prow) in ((0, rbufs[li], "rbuf", 1), (1, ibufs[li], "ibuf", 2)):
                for tb in range(8):
                    p = cd["ps"] % 4
                    cd["ps"] += 1
                    sc.pe(lambda e, p=p, li=li, w=w, tb=tb: e.matmul(psA[:, p, :], lhsT=bd[li][:, w, :], rhs=xcs[li][:, tb * 512:(tb + 1) * 512],
                                                                     start=True, stop=True),
                          reads=[("bd", li), ("xc", li)], writes=[("psA", p)])
                    sc.act(lambda e, p=p, dstb=dstb, tb=tb, prow=prow, c=c: e.activation(
                        out=dstb[:, tb * 512:(tb + 1) * 512], in_=psA[:, p, :], func=AF.Sigmoid, bias=lpar[:, prow, c:c + 1]),
                        reads=[("psA", p), "lpar"], writes=[(nm, li, tb)])
            RB = [("rbuf", li, tb) for tb in range(8)]
            IB = [("ibuf", li, tb) for tb in range(8)]
            return RB, IB

        def d_stage2(c, RB, IB):
            li = c % 2
            sc.act(lambda e, c=c: e.activation(out=rbufs[li][:, :], in_=rbufs[li][:, :], func=AF.Exp, scale=sc8[:, c:c + 1]),
                   reads=RB + ["sc8"], writes=RB)
            sc.dve(lambda e: e.tensor_tensor(out=tbuf[:, :], in0=rbufs[li][:, :], in1=rbufs[li][:, :], op=ALU.mult), reads=RB, writes=["tbuf"])
            sc.act(lambda e: e.activation(out=tbuf[:, :], in_=tbuf[:, :], func=AF.Sqrt, scale=-1.0, bias=1.0), reads=["tbuf"], writes=["tbuf"])
            sc.dve(lambda e: e.tensor_tensor(out=ibufs[li][:, :], in0=ibufs[li][:, :], in1=xcs[li][:, :], op=ALU.mult), reads=IB + [("xc", li)], writes=IB)
            sc.dve(lambda e: e.scalar_tensor_tensor(out=ibufs[li][:, 0:TO], in0=tbuf[:, 0:TO], scalar=hf[:, 0:1], in1=ibufs[li][:, 0:TO],
                                                    op0=ALU.mult, op1=ALU.mult), reads=IB + ["tbuf", "hf"], writes=IB)
            sc.dve(lambda e: e.tensor_tensor(out=ibufs[li][:, TO:T], in0=tbuf[:, TO:T], in1=ibufs[li][:, TO:T], op=ALU.mult),
                   reads=IB + ["tbuf"], writes=IB)
            sc.dve(lambda e: e.tensor_tensor_scan(out=hbuf[:, :], data0=rbufs[li][:, :], data1=ibufs[li][:, :], initial=0.0,
                                                  op0=ALU.mult, op1=ALU.add), reads=RB + IB, writes=["hbuf"])
            sc.dve(lambda e, li=li: e.tensor_tensor(out=gl[:, :], in0=lg[li][:, :], in1=lg[li][:, :], op=ALU.mult),
                   reads=[("lg", li)], writes=["gl"])
            sc.dve(lambda e: e.tensor_scalar(out=gl[:, :], in0=gl[:, :], scalar1=0.044715, scalar2=1.0, op0=ALU.mult, op1=ALU.add),
                   reads=["gl"], writes=["gl"])
            sc.dve(lambda e, li=li: e.tensor_tensor(out=gl[:, :], in0=gl[:, :], in1=lg[li][:, :], op=ALU.mult),
                   reads=["gl", ("lg", li)], writes=["gl"])
            sc.act(lambda e: e.activation(out=gl[:, :], in_=gl[:, :], func=AF.Sigmoid, scale=1.5957691216), reads=["gl"], writes=["gl"])
            sc.dve(lambda e, li=li: e.tensor_tensor(out=gl[:, :], in0=gl[:, :], in1=lg[li][:, :], op=ALU.mult),
                   reads=["gl", ("lg", li)], writes=["gl"])
            sc.dve(lambda e, li=li: e.tensor_tensor(out=lob[li][:, :], in0=gl[:, :], in1=hbuf[:, TO:T], op=ALU.mult),
                   reads=["gl", "hbuf"], writes=[("lob", li)])
            sc.dma(lambda e, li=li, c=c: e.dma_start(out=lruT_s[c], in_=lob[li][:, :]), reads=[("lob", li)], q="gpsimd")
        d_rb = {}
        for c in range(9):
            if c < 8:
                d_rb[c] = d_stage1(c)
            if c >= 1:
                d_stage2(c - 1, *d_rb[c - 1])
        if upto == "D":
            sc.emit()
            pd.close()
            st.close()
            return nc
        sc.emit()

    with ExitStack() as pe_:
        sbe = lambda name, shape, dt=F32: pe_.enter_context(nc.sbuf_tensor("se_" + name, list(shape), dt))
        kS = sbe("kS", [128, 4, T], BF16)
        kW = sbe("kW", [128, 4, T], BF16)
        vS = sbe("vS", [128, NT, 4, 65], BF16)
        vW = sbe("vW", [128, NT, 4, 65], BF16)
        kC = sbe("kC", [128, 4, 256], BF16)
        vC = sbe("vC", [128, 2, 4, 65], BF16)
        esel = sbe("esel", [128, 32, 128], BF16)
        cmask = sbe("cmask", [128, 16, 2, 128], BF16)
        mfar = sbe("mfar", [128, 128], BF16)
        mdiag = sbe("mdiag", [128, 128], BF16)
        bmap = sbe("bmap", [128, 2, 65], BF16)
        bonus = sbe("bonus", [128, 16, 64])
        hb = sbe("hb", [128, 1])
        ones65 = sbe("ones65", [65, 64])
        qp = [sbe(f"qp{i}", [128, 4, 128], BF16) for i in range(2)]
        gsb = [sbe(f"gsb{i}", [65, 3, 4, 128]) for i in range(2)]
        pt = [sbe(f"pt{i}", [128, 512], BF16) for i in range(4)]
        impb = sbe("impb", [128, 64])
        imp2 = sbe("imp2", [128, 64])
        rz = sbe("rz", [128, 4])
        m8a = sbe("m8a", [128, 8])
        m8b = sbe("m8b", [128, 8])
        mnq = sbe("mnq", [128, 64], BF16)
        mneg = [sbe(f"mneg{i}", [128, 4, 128], BF16) for i in range(2)]
        wrow = [sbe(f"wrow{i}", [65, 512]) for i in range(4)]
        wrowb = [sbe(f"wrowb{i}", [65, 512], BF16) for i in range(4)]
        ones65b = sbe("ones65b", [65, 64], BF16)
        bcs = [sbe(f"bcs{i}", [64, 512]) for i in range(2)]
        tmpf = [sbe(f"tmpf{i}", [64, 512]) for i in range(2)]
        accf = [sbe(f"accf{i}", [64, 512]) for i in range(2)]
        aob = [sbe(f"aob{i}", [64, 4, 128], BF16) for i in range(2)]

        sc.pool(lambda e: e.memset(kS[64:128, :, :], 0.0), writes=[("kS", "z")])
        sc.pool(lambda e: e.memset(kW[64:128, :, :], 0.0), writes=[("kW", "z")])
        sc.pool(lambda e: e.memset(kC[64:128, :, :], 0.0), writes=[("kC", "z")])
        for i in range(2):
            sc.dve(lambda e, i=i: e.memset(qp[i][64:128, :, :], 0.0), writes=[("qp", i, "z")])
            sc.dve(lambda e, i=i: e.memset(mneg[i][64:128, :, :], 0.0), writes=[("mneg", i, "z")])
        for (kbuf, si) in ((kS, 0), (kW, 1)):
            nm = "kS" if si == 0 else "kW"
            sc.dma(lambda e, kbuf=kbuf, si=si: e.dma_start(out=kbuf[0:64, :, :], in_=kT_s[si].rearrange("g d t -> d g t")),
                   writes=[(nm, "a")])
            for g in range(4):
                sc.dma(lambda e, kbuf=kbuf, g=g: e.dma_start(out=kbuf[64:71, g, :], in_=kx_in), reads=[(nm, "z")], writes=[(nm, "x", g)])
        for (vbuf, si) in ((vS, 0), (vW, 1)):
            nm = "vS" if si == 0 else "vW"
            sc.pool(lambda e, vbuf=vbuf: e.memset(vbuf[:, :, :, 64:65], 1.0), writes=[(nm, "o")])
            for t in range(NT):
                sc.dma(lambda e, vbuf=vbuf, si=si, t=t: e.dma_start(
                    out=vbuf[:, t, :, 0:64], in_=v_s[si, t * 128:(t + 1) * 128, :].rearrange("p (g d) -> p g d", g=4)),
                    writes=[(nm, t // 4)])
        sc.dma(lambda e: e.dma_start(out=kC[0:64, :, :], in_=kcT_s.rearrange("g d n -> d g n")), writes=[("kC", "a")])
        for g in range(4):
            sc.dma(lambda e, g=g: e.dma_start(out=kC[64:71, g, :], in_=kcx_in), reads=[("kC", "z")], writes=[("kC", "x", g)])
        sc.pool(lambda e: e.memset(vC[:, :, :, 64:65], 1.0), writes=[("vC", "o")])
        for nt in range(2):
            sc.dma(lambda e, nt=nt: e.dma_start(out=vC[:, nt, :, 0:64], in_=vc_s[nt * 128:(nt + 1) * 128, :, :]), writes=[("vC", "a", nt)])
        for (dst, src, nm) in ((esel, esel_in, "esel"), (cmask, cmask_in, "cmask"), (mfar, mfar_in, "mfar"), (mdiag, mdiag_in, "mdiag"),
                               (bmap, bmap_in, "bmap"), (bonus, bonus_in, "bonus"), (hb, hb_in, "hb")):
            if nm == "cmask":
                sc.dma(lambda e, dst=dst, src=src: e.dma_start(out=dst[:, :, :, :].rearrange("p a b c -> p (a b c)"),
                                                               in_=src.rearrange("p a b c -> p (a b c)")), writes=[nm])
            else:
                sc.dma(lambda e, dst=dst, src=src: e.dma_start(out=dst[tuple(slice(None) for _ in dst.shape)], in_=src), writes=[nm])
        sc.dve(lambda e: e.memset(ones65[:, :], 1.0), writes=["ones65"])
        sc.dve(lambda e: e.memset(ones65b[:, :], 1.0), writes=["ones65b"])
        KS_ALL = [("kS", "z"), ("kS", "a")] + [("kS", "x", g) for g in range(4)]
        KW_ALL = [("kW", "z"), ("kW", "a")] + [("kW", "x", g) for g in range(4)]
        KC_ALL = [("kC", "z"), ("kC", "a")] + [("kC", "x", g) for g in range(4)]
        VS_ALL = [("vS", "o")] + [("vS", t4) for t4 in range(8)]
        VW_ALL = [("vW", "o")] + [("vW", t4) for t4 in range(8)]
        VC_ALL = [("vC", "o"), ("vC", "a", 0), ("vC", "a", 1)]
        ce = {"it": 0, "s": 0, "pt": 0, "o": 0, "f": 0}

        def score_tile(lhsT, lhs_keys, qi, masked_add=None, bias=None, mask=None, mask_keys=(), rows=128):
            sl = ce["s"] % 3
            ce["s"] += 1
            pi = ce["pt"] % 4
            ce["pt"] += 1
            sc.pe(lambda e: e.matmul(psA[0:rows, sl, :], lhsT=lhsT, rhs=qp[qi][:, :, :], start=True, stop=(masked_add is None)),
                  reads=list(lhs_keys) + [("qp", qi, "a"), ("qp", qi, "x"), ("qp", qi, "z")], writes=[("psA", sl)])
            if masked_add is not None:
                kt, mi = masked_add
                sc.pe(lambda e: e.matmul(psA[:, sl, :], lhsT=esel[:, kt, :], rhs=mneg[mi][:, :, :], start=False, stop=True),
                      reads=["esel", ("mneg", mi), ("mneg", mi, "z")], writes=[("psA", sl)])
            if bias is None:
                sc.act(lambda e: e.activation(out=pt[pi][0:rows, :], in_=psA[0:rows, sl, :], func=AF.Exp, scale=0.125),
                       reads=[("psA", sl)], writes=[("pt", pi)])
            else:
                sc.act(lambda e: e.activation(out=pt[pi][0:rows, :], in_=psA[0:rows, sl, :], func=AF.Exp, scale=0.125, bias=hb[0:rows, 0:1]),
                       reads=[("psA", sl), "hb"], writes=[("pt", pi)])
            if mask is not None:
                sc.dve(lambda e: e.tensor_tensor(out=pt[pi][0:rows, :].rearrange("k (h q) -> k h q", h=4),
                                                 in0=pt[pi][0:rows, :].rearrange("k (h q) -> k h q", h=4),
                                                 in1=mask.unsqueeze(1).to_broadcast([rows, 4, 128]), op=ALU.mult),
                       reads=[("pt", pi)] + list(mask_keys), writes=[("pt", pi)])
            return pi

        hooks = {}

        def finalize(ob, x, qi, fi, first, last, g, qt, step):
            w4 = ce["f"] % 4
            w = ce["f"] % 2
            ce["f"] += 1
            sc.dve(lambda e: e.tensor_scalar_max(out=wrow[w4][64:65, :], in0=psA[64:65, ob, :], scalar1=1e-30),
                   reads=[("psA", ob)], writes=[("wrow", w4)])
            sc.dve(lambda e: e.reciprocal(out=wrow[w4][64:65, :], in_=wrow[w4][64:65, :]), reads=[("wrow", w4)], writes=[("wrow", w4)])
            sc.dve(lambda e: e.tensor_tensor(out=wrowb[w4][64:65, :], in0=wrow[w4][64:65, :],
                                             in1=gsb[qi][64:65, x, :, :].rearrange("p h q -> p (h q)"), op=ALU.mult),
                   reads=[("wrow", w4), ("gsb", qi, x)], writes=[("wrowb", w4)])

            def part2():
                bsl = ce["s"] % 3
                ce["s"] += 1
                sc.pe(lambda e: e.matmul(psA[0:64, bsl, :], lhsT=ones65b[64:65, :], rhs=wrowb[w4][64:65, :], start=True, stop=True),
                      reads=[("wrowb", w4), "ones65b"], writes=[("psA", bsl)])
                sc.act(lambda e: e.copy(out=bcs[w][:, :], in_=psA[0:64, bsl, :]), reads=[("psA", bsl)], writes=[("bcs", w)])
                if first:
                    sc.dve(lambda e: e.tensor_tensor(out=accf[fi][:, :], in0=psA[0:64, ob, :], in1=bcs[w][:, :], op=ALU.mult),
                           reads=[("psA", ob), ("bcs", w)], writes=[("accf", fi)])
                else:
                    sc.dve(lambda e: e.tensor_tensor(out=tmpf[w][:, :], in0=psA[0:64, ob, :], in1=bcs[w][:, :], op=ALU.mult),
                           reads=[("psA", ob), ("bcs", w)], writes=[("tmpf", w)])
                    if not last:
                        sc.pool(lambda e: e.tensor_tensor(out=accf[fi][:, :], in0=accf[fi][:, :], in1=tmpf[w][:, :], op=ALU.add),
                                reads=[("accf", fi), ("tmpf", w)], writes=[("accf", fi)])
                    else:
                        sc.pool(lambda e: e.tensor_tensor(out=aob[fi][:, :, :].rearrange("d h q -> d (h q)"), in0=accf[fi][:, :],
                                                          in1=tmpf[w][:, :], op=ALU.add),
                                reads=[("accf", fi), ("tmpf", w)], writes=[("aob", fi)])
                        sc.dma(lambda e: e.dma_start(out=attnT_s[g * 4:(g + 1) * 4, :, qt * 128:(qt + 1) * 128].rearrange("h d q -> d h q"),
                                                     in_=aob[fi][:, :, :]), reads=[("aob", fi)], q="gpsimd")
            hooks.setdefault(step + DEFER[0 if first else (2 if last else 1)], []).append(part2)

        jobs = []

        jl = {}

        def add_job(s1, s2, it, br):
            jl.setdefault((it, br), []).append((s1, s2))

        for qt in range(QT_LIMIT if QT_LIMIT else 16):
            vt = 16 + qt
            for g in range(4):
                it = ce["it"]
                ce["it"] += 1
                qi = it % 2
                fi = it % 2
                mi = it % 2
                crows = [128, 8 * (vt + 1) - 128]
                st_ = {"pis": [None, None]}

                def loads(qi=qi, g=g, qt=qt):
                    sc.dma(lambda e: e.dma_start(
                        out=qp[qi][0:64, :, :], in_=qT_s[g * 4:(g + 1) * 4, :, qt * 128:(qt + 1) * 128].rearrange("h d q -> d h q")),
                        writes=[("qp", qi, "a")])
                    sc.dma(lambda e: e.dma_start(
                        out=qp[qi][64:71, :, :], in_=qx_in[g * 4:(g + 1) * 4, :, qt * 128:(qt + 1) * 128].rearrange("h r q -> r h q")),
                        reads=[("qp", qi, "z")], writes=[("qp", qi, "x")])
                    for x in range(3):
                        sc.dma(lambda e, x=x: e.dma_start(
                            out=gsb[qi][64:65, x, :, :],
                            in_=gate_s[g * 12:(g + 1) * 12, qt * 128:(qt + 1) * 128].rearrange("(o h x) q -> o x h q", o=1, x=3)[:, x, :, :]),
                            writes=[("gsb", qi, x)])

                ob_c = 3 + ce["o"] % 3
                ce["o"] += 1
                for nt in range(2):
                    def s1(nt=nt, g=g, qi=qi, qt=qt, st_=st_, crows=crows, loads=loads):
                        if nt == 0:
                            loads()
                        rw = crows[nt]
                        st_["pis"][nt] = score_tile(kC[:, g, nt * 128:nt * 128 + rw], KC_ALL, qi, bias=(True if nt == 0 else None),
                                                    mask=cmask[0:rw, qt, nt, :], mask_keys=["cmask"], rows=rw)

                    def s2(step, nt=nt, g=g, qi=qi, qt=qt, fi=fi, mi=mi, st_=st_, crows=crows, ob=ob_c):
                        rw = crows[nt]
                        pi = st_["pis"][nt]
                        sc.pe(lambda e: e.matmul(psA[0:65, ob, :], lhsT=vC[0:rw, nt, g, :], rhs=pt[pi][0:rw, :],
                                                 start=(nt == 0), stop=(nt == 1)),
                              reads=VC_ALL + [("pt", pi)], writes=[("psA", ob)])
                        if nt == 0:
                            return
                        pis = st_["pis"]
                        for h in range(4):
                            for n2 in range(2):
                                r2 = crows[n2]
                                sc.pe(lambda e, h=h, n2=n2, r2=r2: e.matmul(psT_f[:, h, :], lhsT=pt[pis[n2]][0:r2, h * 128:(h + 1) * 128],
                                                                            rhs=bmap[0:r2, n2, :], start=(n2 == 0), stop=(n2 == 1)),
                                      reads=[("pt", pis[n2]), "bmap"], writes=["psR"])
                        sc.dve(lambda e: e.tensor_scalar_max(out=rz[:, :], in0=psT_f[:, :, 64], scalar1=1e-30), reads=["psR"], writes=["rz"])
                        sc.dve(lambda e: e.reciprocal(out=rz[:, :], in_=rz[:, :]), reads=["rz"], writes=["rz"])
                        for h in range(4):
                            sc.dve(lambda e, h=h: e.scalar_tensor_tensor(
                                out=impb[:, :], in0=psT_f[:, h, 0:64], scalar=rz[:, h:h + 1],
                                in1=(bonus[:, qt, :] if h == 0 else impb[:, :]), op0=ALU.mult, op1=ALU.add),
                                reads=["psR", "rz", "bonus", "impb"], writes=["impb"])
                        sc.dve(lambda e: e.max(out=m8a[:, :], in_=impb[:, :]), reads=["impb"], writes=["m8a"])
                        sc.dve(lambda e: e.match_replace(out=imp2[:, :], in_to_replace=m8a[:, :], in_values=impb[:, :], imm_value=-3.0e38),
                               reads=["impb", "m8a"], writes=["imp2"])
                        sc.dve(lambda e: e.max(out=m8b[:, :], in_=imp2[:, :]), reads=["imp2"], writes=["m8b"])
                        sc.dve(lambda e: e.tensor_scalar_max(out=m8b[:, 7:8], in0=m8b[:, 7:8], scalar1=-1.0e29), reads=["m8b"], writes=["m8b"])
                        sc.dve(lambda e: e.tensor_scalar(out=mnq[:, :], in0=impb[:, :], scalar1=m8b[:, 7:8], scalar2=-4096.0,
                                                         op0=ALU.is_lt, op1=ALU.mult), reads=["impb", "m8b"], writes=["mnq"])
                        def tpose():
                            sc.pe(lambda e: e.transpose(out=psT[0:64, 0, 0:128], in_=mnq[:, :], identity=ident[:, :]),
                                  reads=["mnq", "ident"], writes=["psTm"])
                            sc.act(lambda e: e.copy(out=mneg[mi][0:64, :, :], in_=psT[0:64, 0, 0:128].unsqueeze(1).to_broadcast([64, 4, 128])),
                                   reads=["psTm", ("mneg", mi, "z")], writes=[("mneg", mi)])
                        hooks.setdefault(step + 4, []).append(tpose)
                        finalize(ob, 0, qi, fi, True, False, g, qt, step)
                    add_job(s1, s2, it, "c")
                ob_w = 3 + ce["o"] % 3
                ce["o"] += 1
                for r in range(5):
                    kt = vt - 4 + r
                    st_w = {}

                    def s1(r=r, kt=kt, g=g, qi=qi, st_w=st_w):
                        mask, mk = (mfar[:, :], ["mfar"]) if r == 0 else ((mdiag[:, :], ["mdiag"]) if r == 4 else (None, ()))
                        st_w["pi"] = score_tile(kW[:, g, kt * 128:(kt + 1) * 128], KW_ALL, qi, bias=(True if kt < 16 else None),
                                                mask=mask, mask_keys=mk)

                    def s2(step, r=r, kt=kt, g=g, qi=qi, qt=qt, fi=fi, st_w=st_w, ob=ob_w):
                        pi = st_w["pi"]
                        sc.pe(lambda e: e.matmul(psA[0:65, ob, :], lhsT=vW[:, kt, g, :], rhs=pt[pi][:, :], start=(r == 0), stop=(r == 4)),
                              reads=VW_ALL + [("pt", pi)], writes=[("psA", ob)])
                        if r == 4:
                            finalize(ob, 2, qi, fi, False, False, g, qt, step)
                    add_job(s1, s2, it, "w")
                ob_s = 3 + ce["o"] % 3
                ce["o"] += 1
                for kt in range(vt + 1):
                    st_s = {}

                    def s1(kt=kt, vt=vt, g=g, qi=qi, mi=mi, st_s=st_s):
                        mask, mk = (mdiag[:, :], ["mdiag"]) if kt == vt else (None, ())
                        st_s["pi"] = score_tile(kS[:, g, kt * 128:(kt + 1) * 128], KS_ALL, qi, masked_add=(kt, mi), mask=mask, mask_keys=mk)

                    def s2(step, kt=kt, vt=vt, g=g, qi=qi, qt=qt, fi=fi, st_s=st_s, ob=ob_s):
                        pi = st_s["pi"]
                        sc.pe(lambda e: e.matmul(psA[0:65, ob, :], lhsT=vS[:, kt, g, :], rhs=pt[pi][:, :], start=(kt == 0), stop=(kt == vt)),
                              reads=VS_ALL + [("pt", pi)], writes=[("psA", ob)])
                        if kt == vt:
                            finalize(ob, 1, qi, fi, False, True, g, qt, step)
                    add_job(s1, s2, it, "s")
        n_it = ce["it"]
        KH = 12
        jobs = list(jl[(0, "c")]) + list(jl[(0, "w")])
        for it in range(n_it):
            sj = jl[(it, "s")]
            jobs += sj[:-KH]
            if it + 1 < n_it:
                jobs += jl[(it + 1, "c")]
            jobs += sj[-KH:]
            if it + 1 < n_it:
                jobs += jl[(it + 1, "w")]
        LA = 2
        for i in range(len(jobs) + LA + 8):
            for hk in hooks.get(i, ()):
                hk()
            if i < len(jobs):
                jobs[i][0]()
            if 0 <= i - LA < len(jobs):
                jobs[i - LA][1](i)
        if upto == "E":
            sc.emit()
            pe_.close()
            st.close()
            return nc
        sc.emit()

    with ExitStack() as pf:
        sbf = lambda name, shape, dt=F32: pf.enter_context(nc.sbuf_tensor("sf_" + name, list(shape), dt))
        h2T = sbf("h2T", [128, 8, TO], BF16)
        g2 = sbf("g2", [128, 8])
        sc.dma(lambda e: e.dma_start(out=g2[:, :], in_=g2_in), writes=["g2"])
        with ExitStack() as pfa:
            sfa = lambda name, shape, dt=F32: pfa.enter_context(nc.sbuf_tensor("sfa_" + name, list(shape), dt))
            wst = [sfa(f"wst{i}", [128, 8, 512]) for i in range(2)]
            Wb = [sfa(f"Wb{i}", [128, 8, D], BF16) for i in range(3)]
            aT = [sfa(f"aT{i}", [128, 8, 128], BF16) for i in range(2)]
            lT = [sfa(f"lT{i}", [128, 8, 128], BF16) for i in range(2)]
            mgt = [sfa(f"mgt{i}", [128, 2048]) for i in range(2)]
            xt2 = [sfa(f"xt2{i}", [128, D]) for i in range(2)]
            m1 = sfa("m1", [128, D])
            m2 = sfa("m2", [128, D])
            mbs = [sfa(f"mb{i}", [128, D], BF16) for i in range(2)]
            mT = sfa("mT", [128, 8, 128], BF16)
            x1t = [sfa(f"x1t{i}", [128, D]) for i in range(2)]
            sq2 = sfa("sq2", [128, D])
            ss2 = sfa("ss2", [128, 1])
            xn2 = sfa("xn2", [128, D], BF16)
            cf = {"w": 0}
            for wi, src in enumerate((woa_in, wol_in, wout_in)):
                srcv = src.rearrange("(kc p) n -> p kc n", p=128)
                for nb in range(2):
                    i = cf["w"] % 2
                    cf["w"] += 1
                    sc.dma(lambda e, i=i, srcv=srcv, nb=nb: e.dma_start(out=wst[i][:, :, :], in_=srcv[:, :, nb * 512:(nb + 1) * 512]),
                           writes=[("wst", i)])
                    sc.pool(lambda e, i=i, wi=wi, nb=nb: e.tensor_copy(out=Wb[wi][:, :, nb * 512:(nb + 1) * 512], in_=wst[i][:, :, :]),
                            reads=[("wst", i)], writes=[("Wb", wi, nb)])
            attn_v = attnT_s.rearrange("h d q -> (h d) q").rearrange("(kc p) q -> p kc q", p=128)
            def stage_x(t):
                i = t % 2
                sc.dma(lambda e, i=i, t=t: e.dma_start(out=aT[i][:, :, :], in_=attn_v[:, :, t * 128:(t + 1) * 128]), writes=[("aT", i)])
                sc.dma(lambda e, i=i, t=t: e.dma_start(out=lT[i][:, :, :], in_=lruT_s[:, :, t * 128:(t + 1) * 128].rearrange("c p q -> p c q")),
                       writes=[("lT", i)])
                sc.dma(lambda e, i=i, t=t: e.dma_start(out=mgt[i][:, :], in_=mg_s[t * 128:(t + 1) * 128, :]), writes=[("mgt", i)])
                sc.dma(lambda e, i=i, t=t: e.dma_start(out=xt2[i][:, :], in_=xv[TO + t * 128:TO + (t + 1) * 128, :]), writes=[("xt2", i)])
                for (wi, src_t, nm, base) in ((0, aT, "aT", 0), (1, lT, "lT", 2)):
                    for nb in range(2):
                        for kc in range(8):
                            sc.pe(lambda e, wi=wi, src_t=src_t, nb=nb, kc=kc, base=base, i=i: e.matmul(
                                psA[:, base + nb, :], lhsT=src_t[i][:, kc, :], rhs=Wb[wi][:, kc, nb * 512:(nb + 1) * 512],
                                start=(kc == 0), stop=(kc == 7)),
                                reads=[(nm, i), ("Wb", wi, nb)], writes=[("psA", base + nb)])
                sc.dve(lambda e, i=i: e.tensor_tensor(out=m1[:, :].rearrange("p (a b) -> p a b", a=2), in0=psA[:, 0:2, :],
                                                      in1=mgt[i][:, 0:D].rearrange("p (a b) -> p a b", a=2), op=ALU.mult),
                       reads=[("psA", 0), ("psA", 1), ("mgt", i)], writes=["m1"])
                sc.dve(lambda e, i=i: e.tensor_tensor(out=m2[:, :].rearrange("p (a b) -> p a b", a=2), in0=psA[:, 2:4, :],
                                                      in1=mgt[i][:, D:2 * D].rearrange("p (a b) -> p a b", a=2), op=ALU.mult),
                       reads=[("psA", 2), ("psA", 3), ("mgt", i)], writes=["m2"])
                sc.dve(lambda e, i=i: e.tensor_tensor(out=mbs[i][:, :], in0=m1[:, :], in1=m2[:, :], op=ALU.add), reads=["m1", "m2"], writes=[("mb", i)])

            def stage_y(t):
                i = t % 2
                for kc in range(8):
                    sc.pe(lambda e, kc=kc, i=i: e.transpose(out=psT[:, 0, kc * 128:(kc + 1) * 128], in_=mbs[i][:, kc * 128:(kc + 1) * 128],
                                                       identity=ident[:, :]), reads=[("mb", i), "ident"], writes=[("psT", 0)])
                sc.act(lambda e: e.copy(out=mT[:, :, :], in_=psT[:, 0, :].rearrange("p (k n) -> p k n", k=8)),
                       reads=[("psT", 0)], writes=["mT"])
                for nb in range(2):
                    for kc in range(8):
                        sc.pe(lambda e, nb=nb, kc=kc: e.matmul(psA[:, 4 + nb, :], lhsT=mT[:, kc, :], rhs=Wb[2][:, kc, nb * 512:(nb + 1) * 512],
                                                               start=(kc == 0), stop=(kc == 7)),
                              reads=["mT", ("Wb", 2, nb)], writes=[("psA", 4 + nb)])
                sc.dve(lambda e, i=i: e.tensor_tensor(out=x1t[i][:, :].rearrange("p (a b) -> p a b", a=2), in0=psA[:, 4:6, :],
                                                      in1=xt2[i][:, :].rearrange("p (a b) -> p a b", a=2), op=ALU.add),
                       reads=[("psA", 4), ("psA", 5), ("xt2", i)], writes=[("x1t", i)])
                sc.dma(lambda e, i=i, t=t: e.dma_start(out=x1_s[t * 128:(t + 1) * 128, :], in_=x1t[i][:, :]), reads=[("x1t", i)], q="gpsimd",
                       writes=[("x1_s", t)])
                sc.dve(lambda e, i=i: e.tensor_tensor(out=sq2[:, :], in0=x1t[i][:, :], in1=x1t[i][:, :], op=ALU.mult),
                       reads=[("x1t", i)], writes=["sq2"])
                sc.dve(lambda e: e.tensor_reduce(out=ss2[:, :], in_=sq2[:, :], axis=AX.X, op=ALU.add), reads=["sq2"], writes=["ss2"])
                sc.dve(lambda e: e.tensor_scalar(out=ss2[:, :], in0=ss2[:, :], scalar1=1.0 / D, scalar2=EPS, op0=ALU.mult, op1=ALU.add),
                       reads=["ss2"], writes=["ss2"])
                sc.dve(lambda e: e.reciprocal(out=ss2[:, :], in_=ss2[:, :]), reads=["ss2"], writes=["ss2"])
                sc.act(lambda e: e.activation(out=ss2[:, :], in_=ss2[:, :], func=AF.Sqrt), reads=["ss2"], writes=["ss2"])
                sc.act(lambda e, i=i: e.mul(out=xn2[:, :], in_=x1t[i][:, :], mul=ss2[:, 0:1]), reads=[("x1t", i), "ss2"], writes=["xn2"])
                for kc in range(8):
                    sc.pe(lambda e, kc=kc: e.transpose(out=psT[:, 0, kc * 128:(kc + 1) * 128], in_=xn2[:, kc * 128:(kc + 1) * 128],
                                                       identity=ident[:, :]), reads=["xn2", "ident"], writes=[("psT", 0)])
                sc.dve(lambda e, t=t: e.tensor_copy(out=h2T[:, :, t * 128:(t + 1) * 128], in_=psT[:, 0, :].rearrange("p (k n) -> p k n", k=8)),
                       reads=[("psT", 0)], writes=[("h2T", t // 4)])
            for t in range(17):
                if t < 16:
                    stage_x(t)
                if t >= 1:
                    stage_y(t - 1)
            if upto == "Fa":
                sc.emit()
                pfa.close()
                pf.close()
                st.close()
                return nc
        with ExitStack() as pfb:
            sfb = lambda name, shape, dt=F32: pfb.enter_context(nc.sbuf_tensor("sfb_" + name, list(shape), dt))
            wst = [sfb(f"wst{i}", [128, 8, 512]) for i in range(2)]
            Wd = sfb("Wd", [128, 22, D], BF16)
            actT = sfb("actT", [128, 22, 1024], BF16)
            wgf = [sfb(f"wgf{i}", [128, 2, 8, 128]) for i in range(2)]
            wgb = [sfb(f"wgb{i}", [128, 2, 8, 128], BF16) for i in range(2)]
            sg = [sfb(f"sg{i}", [128, 512]) for i in range(2)]
            x1r = [sfb(f"x1r{i}", [128, D]) for i in range(2)]
            yt = [sfb(f"yt{i}", [128, D]) for i in range(2)]
            wd_v = wd_in.rearrange("(fc p) n -> p fc n", p=128)
            cf = {"w": 0, "g": 0, "ps": 0, "s": 0}
            for f0 in range(0, 22, 4):
                nf = min(4, 22 - f0)
                for nb in range(2):
                    i = cf["w"] % 2
                    cf["w"] += 1
                    sc.dma(lambda e, i=i, f0=f0, nf=nf, nb=nb: e.dma_start(out=wst[i][:, 0:nf, :], in_=wd_v[:, f0:f0 + nf, nb * 512:(nb + 1) * 512]),
                           writes=[("wst", i)])
                    sc.pool(lambda e, i=i, f0=f0, nf=nf, nb=nb: e.tensor_copy(out=Wd[:, f0:f0 + nf, nb * 512:(nb + 1) * 512], in_=wst[i][:, 0:nf, :]),
                            reads=[("wst", i)], writes=[("Wd", f0, nb)])
            WD_ALL = [("Wd", f0, nb) for f0 in range(0, 22, 4) for nb in range(2)]
            wg_v = wg_in.rearrange("(kc p) n -> p kc n", p=128)
            wu_v = wu_in.rearrange("(kc p) n -> p kc n", p=128)
            for half in range(2):
                for fcn in range(22):
                    gi = cf["g"] % 2
                    cf["g"] += 1
                    sc.dma(lambda e, gi=gi, fcn=fcn: e.dma_start(out=wgf[gi][:, 0, :, :], in_=wg_v[:, :, fcn * 128:(fcn + 1) * 128]),
                           writes=[("wgf", gi, 0)])
                    sc.dma(lambda e, gi=gi, fcn=fcn: e.dma_start(out=wgf[gi][:, 1, :, :], in_=wu_v[:, :, fcn * 128:(fcn + 1) * 128]),
                           writes=[("wgf", gi, 1)])
                    for w in range(2):
                        sc.pool(lambda e, gi=gi, w=w: e.tensor_tensor(out=wgb[gi][:, w, :, :], in0=wgf[gi][:, w, :, :],
                                                                      in1=g2[:, :].unsqueeze(2).to_broadcast([128, 8, 128]), op=ALU.mult),
                                reads=[("wgf", gi, w), "g2"], writes=[("wgb", gi, w)])
                    for tb in range(2):
                        tok0 = half * 1024 + tb * 512
                        pg = cf["ps"] % 6
                        pu = (cf["ps"] + 1) % 6
                        cf["ps"] += 2
                        for (w, pp) in ((0, pg), (1, pu)):
                            for kc in range(8):
                                sc.pe(lambda e, w=w, pp=pp, kc=kc, gi=gi, tok0=tok0: e.matmul(
                                    psA[:, pp, :], lhsT=wgb[gi][:, w, kc, :], rhs=h2T[:, kc, tok0:tok0 + 512], start=(kc == 0), stop=(kc == 7)),
                                    reads=[("wgb", gi, w), ("h2T", tok0 // 512)], writes=[("psA", pp)])
                        si = cf["s"] % 2
                        cf["s"] += 1
                        sc.act(lambda e, si=si, pg=pg: e.activation(out=sg[si][:, :], in_=psA[:, pg, :], func=AF.Silu),
                               reads=[("psA", pg)], writes=[("sg", si)])
                        sc.dve(lambda e, si=si, pu=pu, fcn=fcn, tb=tb: e.tensor_tensor(out=actT[:, fcn, tb * 512:(tb + 1) * 512], in0=sg[si][:, :],
                                                                                       in1=psA[:, pu, :], op=ALU.mult),
                               reads=[("sg", si), ("psA", pu)], writes=[("actT", fcn)])
                for tl in range(8):
                    t = half * 8 + tl
                    i = t % 2
                    sc.dma(lambda e, i=i, t=t: e.dma_start(out=x1r[i][:, :], in_=x1_s[t * 128:(t + 1) * 128, :]), reads=[("x1_s", t)],
                           writes=[("x1r", i)])
                    pbase = (cf["ps"] % 3) * 2
                    cf["ps"] += 2
                    for nb in range(2):
                        for fcn in range(22):
                            sc.pe(lambda e, nb=nb, fcn=fcn, tl=tl, pbase=pbase: e.matmul(
                                psA[:, pbase + nb, :], lhsT=actT[:, fcn, tl * 128:(tl + 1) * 128], rhs=Wd[:, fcn, nb * 512:(nb + 1) * 512],
                                start=(fcn == 0), stop=(fcn == 21)),
                                reads=[("actT", fcn)] + WD_ALL, writes=[("psA", pbase + nb)])
                    sc.dve(lambda e, i=i, pbase=pbase: e.tensor_tensor(out=yt[i][:, :].rearrange("p (a b) -> p a b", a=2),
                                                                       in0=psA[:, pbase:pbase + 2, :],
                                                                       in1=x1r[i][:, :].rearrange("p (a b) -> p a b", a=2), op=ALU.add),
                           reads=[("psA", pbase), ("psA", pbase + 1), ("x1r", i)], writes=[("yt", i)])
                    tk = sc.dma(lambda e, i=i, t=t: e.dma_start(out=y[t * 128:(t + 1) * 128, :], in_=yt[i][:, :]), reads=[("yt", i)], q="gpsimd")
                    last_out_tokens.append(tk)
        sc.emit(last_out_tokens)

    st.close()
    return nc


def _split_bf16(x, n):
    x = np.asarray(x, np.float64)
    parts = []
    r = x.copy()
    for _ in range(n):
        p = r.astype(np.float32).astype(NPBF)
        parts.append(p)
        r = r - p.astype(np.float64)
    return parts


def host_consts(half):
    c = {}
    c["ident"] = np.eye(128, dtype=np.float32).astype(NPBF)
    t = np.arange(T)
    a, b = t // 64, t % 64
    c["kx"] = np.stack([a, a, a, b - 32, b - 32, np.ones(T), np.ones(T)]).astype(np.float32).astype(NPBF)
    te = 16 * np.arange(256) + 31
    a, b = te // 64, te % 64
    c["kcx"] = np.stack([a, a, a, b - 32, b - 32, np.ones(256), np.ones(256)]).astype(np.float32).astype(NPBF)
    slopes = np.power(2.0, -8.0 * np.arange(1, 17) / 16)
    qx = np.zeros((16, 7, TO), np.float32).astype(NPBF)
    tc = (16 + np.arange(TO) // 128) * 128 + 64
    for h in range(16):
        m = slopes[h]
        A = _split_bf16(np.full(TO, 512.0 * m), 3)
        Bp = _split_bf16(np.full(TO, 8.0 * m), 2)
        C = _split_bf16(-8.0 * m * (tc - 32), 2)
        for r, p in enumerate(A + Bp + C):
            qx[h, r] = p
    c["qx"] = qx
    kk = np.arange(32 * 128).reshape(32, 128)
    es = np.zeros((128, 32, 128), np.float32)
    es[:64] = (kk[None, :, :] // 64 == np.arange(64)[:, None, None])
    c["esel"] = es.astype(NPBF)
    n = np.arange(128)[:, None, None, None] + 128 * np.arange(2)[None, None, :, None]
    tq = (16 + np.arange(16))[None, :, None, None] * 128 + np.arange(128)[None, None, None, :]
    c["cmask"] = (16 * n + 31 <= tq).astype(np.float32).astype(NPBF)
    ik, iq = np.arange(128)[:, None], np.arange(128)[None, :]
    c["mfar"] = (iq < ik).astype(np.float32).astype(NPBF)
    c["mdiag"] = (iq >= ik).astype(np.float32).astype(NPBF)
    cs = np.arange(255) * 16
    ce_ = cs + 31
    bs = np.arange(64) * 64
    be = bs + 63
    bm = ((cs[:, None] <= be[None, :]) & (ce_[:, None] >= bs[None, :])).astype(np.float32)
    bmap = np.zeros((256, 65), np.float32)
    bmap[:255, :64] = bm
    bmap[:, 64] = 1.0
    c["bmap"] = np.ascontiguousarray(bmap.reshape(2, 128, 65).transpose(1, 0, 2)).astype(NPBF)
    tqv = (16 + np.arange(16))[None, :, None] * 128 + np.arange(128)[:, None, None]
    j = np.arange(64)[None, None, :]
    first = 0 if half == 1 else 32
    cur = tqv // 64
    valid = (j * 64 <= tqv) & (j >= first)
    forced = (j == first) | (j == cur) | (j == cur - 1)
    c["bonus"] = np.where(valid, 1e4 * forced, -1e30).astype(np.float32)
    c["hb"] = np.full((128, 1), 0.0 if half == 1 else -30000.0, np.float32)
    c["hf"] = np.full((128, 1), 1.0 if half == 1 else 0.0, np.float32)
    return c


def make_in_maps(inputs):
    x = np.asarray(inputs["x"], dtype=np.float32)
    consts = [host_consts(0), host_consts(1)]
    g1 = np.ascontiguousarray(np.asarray(inputs["norm1_g"], np.float32)[0].reshape(8, 128).T)
    qg = np.ascontiguousarray(np.concatenate([np.asarray(inputs["q_norm_g"], np.float32)[0][None, :],
                                              np.asarray(inputs["k_norm_g"], np.float32)[0]], axis=0).T)
    w_in = np.ascontiguousarray(np.asarray(inputs["w_in"], np.float32)[0])
    f32 = lambda k: np.asarray(inputs[k], np.float32)[0]
    shared = {}
    shared["posT"] = np.ascontiguousarray(np.stack([f32("cmp_pos_k").T, f32("cmp_pos_v").T]))
    shared["cw1"] = np.ascontiguousarray(np.stack([f32("cmp_w1_k"), f32("cmp_w1_v")]))
    shared["cb1"] = np.ascontiguousarray(np.stack([f32("cmp_b1_k").reshape(2, 128).T, f32("cmp_b1_v").reshape(2, 128).T]))
    shared["cw2"] = np.ascontiguousarray(np.stack([f32("cmp_w2_k").reshape(2, 128, 64).transpose(1, 0, 2),
                                                   f32("cmp_w2_v").reshape(2, 128, 64).transpose(1, 0, 2)]))
    shared["cb2c"] = np.ascontiguousarray(np.stack([f32("cmp_b2_k").reshape(64, 1), f32("cmp_b2_v").reshape(64, 1)]))
    shared["cb2r"] = np.ascontiguousarray(np.stack([f32("cmp_b2_k").reshape(1, 64), f32("cmp_b2_v").reshape(1, 64)]))
    cw = f32("conv_w")[:, 0, :]
    shared["lcw"] = np.ascontiguousarray(cw.reshape(4, 8, 128).transpose(2, 1, 0))
    shared["lpar"] = np.ascontiguousarray(np.stack([f32("conv_b"), f32("lru_ba"), f32("lru_bi"), f32("lru_lambda")])
                                          .reshape(4, 8, 128).transpose(2, 0, 1))
    lbd = np.zeros((2, 8, 128, 128), np.float32)
    for w, key in enumerate(("lru_wa", "lru_wi")):
        ww = f32(key)
        for c in range(8):
            lbd[w, c, 0:64, 0:64] = ww[2 * c]
            lbd[w, c, 64:128, 64:128] = ww[2 * c + 1]
    shared["lbd"] = lbd
    for k in ("w_o_attn", "w_o_lru", "w_out", "w_gate", "w_up", "w_down"):
        shared[k] = np.ascontiguousarray(f32(k))
    shared["g2"] = np.ascontiguousarray(f32("norm2_g").reshape(8, 128).T)
    maps = []
    for c in range(8):
        b, half = c // 2, c % 2
        xvv = np.zeros((T, D), np.float32)
        if half == 1:
            xvv[:] = x[b]
        else:
            xvv[TO:] = x[b, :TO]
        m = {"xv": xvv, "w_in": w_in, "g1": g1, "qg": qg}
        m.update(consts[half])
        m.update(shared)
        maps.append(m)
    return maps


def kernel(**inputs):
    nc = build_nc()
    maps = make_in_maps(inputs)
    res = run_bass_kernel_spmd(nc, maps, core_ids=list(range(8)))
    out = np.zeros((4, 4096, D), np.float32)
    for c in range(8):
        b, half = c // 2, c % 2
        out[b, half * TO:(half + 1) * TO] = np.asarray(res.results[c]["y"], np.float32)
    return out
```
